# Optimizing a Trainium2 kernel written in Bass

```python
import math
import jax, jax.numpy as jnp
from jax import lax
import numpy as np

D_MODEL = 1024
BATCH = 2
SEQ = 8192
DEPTH = 1

ATTN_WIDTH = D_MODEL // 2
ATTN_HEAD_DIM = 64
N_ATTN_HEADS = ATTN_WIDTH // ATTN_HEAD_DIM
MLSTM_WIDTH = D_MODEL - ATTN_WIDTH
N_MLSTM_HEADS = 4
MLSTM_HEAD_DIM = MLSTM_WIDTH // N_MLSTM_HEADS
MLSTM_CHUNK = 64
CONV_WIDTH = 4
DILATED_PATTERNS = ((128, 1), (512, 4), (2048, 16))
ATTN_BLOCK = 128
ATTN_SCALE = ATTN_HEAD_DIM ** -0.5
D_FF = ((8 * D_MODEL // 3 + 255) // 256) * 256
N_MOD = 6
RMS_EPS = 1e-6
IN_SPLITS = (ATTN_WIDTH, ATTN_WIDTH, ATTN_WIDTH,
             MLSTM_WIDTH, MLSTM_WIDTH, MLSTM_WIDTH,
             MLSTM_WIDTH,
             N_MLSTM_HEADS, N_MLSTM_HEADS)
IN_COLS = sum(IN_SPLITS)

kernel_name = "hymba_dilated_attn_mlstm_adaln_block"


def rms_norm(x, g):
    xf = x.astype(jnp.float32)
    y = xf * lax.rsqrt(jnp.mean(xf * xf, axis=-1, keepdims=True) + RMS_EPS)
    return (y * g.astype(jnp.float32)).astype(x.dtype)


def modulate(h, shift, scale):
    return h * (1 + scale[:, None, :]) + shift[:, None, :]


def causal_short_conv(x, w, b):
    S = x.shape[1]
    xp = jnp.pad(x, ((0, 0), (CONV_WIDTH - 1, 0), (0, 0)))
    y = b
    for j in range(CONV_WIDTH):
        y = y + xp[:, j:j + S] * w[j]
    return y


def dilated_branch(q, k, v, window, dilation):
    B, S, H, Dh = q.shape
    n_back = window // dilation
    assert n_back <= ATTN_BLOCK and S % dilation == 0
    L = S // dilation
    nb = -(-L // ATTN_BLOCK)
    Lp = nb * ATTN_BLOCK

    def to_sub(t):
        t = t.reshape(B, L, dilation, H, Dh).transpose(0, 2, 1, 3, 4)
        t = jnp.pad(t, ((0, 0), (0, 0), (0, Lp - L), (0, 0), (0, 0)))
        return t.reshape(B, dilation, nb, ATTN_BLOCK, H, Dh)

    def with_prev(t):
        prev = jnp.pad(t, ((0, 0), (0, 0), (1, 0), (0, 0), (0, 0), (0, 0)))[:, :, :-1]
        return jnp.concatenate([prev, t], axis=3)

    qb = to_sub(q)
    kb = with_prev(to_sub(k))
    vb = with_prev(to_sub(v))
    s = jnp.einsum('brnqhd,brnkhd->brnhqk', qb, kb).astype(jnp.float32)
    qi = jnp.arange(ATTN_BLOCK)[:, None] + ATTN_BLOCK
    kj = jnp.arange(2 * ATTN_BLOCK)[None, :]
    rel = qi - kj
    band = (rel >= 0) & (rel <= n_back)
    valid = band[None] & ((jnp.arange(nb)[:, None, None] > 0) | (kj[None] >= ATTN_BLOCK))
    s = jnp.where(valid[None, None, :, None], s, -jnp.inf)
    lse = jax.nn.logsumexp(s, axis=-1)
    p = jnp.exp(s - lse[..., None])
    o = jnp.einsum('brnhqk,brnkhd->brnqhd', p.astype(v.dtype), vb)
    o = o.reshape(B, dilation, Lp, H, Dh)[:, :, :L]
    o = o.transpose(0, 2, 1, 3, 4).reshape(B, S, H, Dh)
    lse = lse.transpose(0, 1, 2, 4, 3).reshape(B, dilation, Lp, H)[:, :, :L]
    lse = lse.transpose(0, 2, 1, 3).reshape(B, S, H)
    return o, lse


def dilated_attention(q, k, v):
    outs, lses = [], []
    for window, dilation in DILATED_PATTERNS:
        o, lse = dilated_branch(q, k, v, window, dilation)
        outs.append(o)
        lses.append(lse)
    w = jax.nn.softmax(jnp.stack(lses, axis=0), axis=0)
    return jnp.einsum('pbsh,pbshd->bshd', w.astype(v.dtype), jnp.stack(outs, axis=0))


def mlstm_chunkwise(q, k, v, log_i, log_f):
    B, NH, S, D = q.shape
    L = MLSTM_CHUNK
    NC = S // L
    k = k * (D ** -0.5)

    def to_chunks(t):
        return jnp.moveaxis(t.reshape(B, NH, NC, L, *t.shape[3:]), 2, 0)

    xs = (to_chunks(q), to_chunks(k), to_chunks(v), to_chunks(log_i), to_chunks(log_f))
    causal = jnp.tril(jnp.ones((L, L), dtype=bool))

    def step(carry, inp):
        C, n, m = carry
        qc, kc, vc, ic, fc = inp
        b = jnp.cumsum(fc, axis=-1)
        log_d = b[..., :, None] - b[..., None, :] + ic[..., None, :]
        log_d = jnp.where(causal, log_d, -jnp.inf)
        m_inter = b + m[..., None]
        m_t = jnp.maximum(m_inter, jnp.max(log_d, axis=-1))
        d_mat = jnp.exp(log_d - m_t[..., None])
        inter = jnp.exp(m_inter - m_t)
        s_qk = jnp.einsum('bhtd,bhsd->bhts', qc, kc) * d_mat
        num = inter[..., None] * jnp.einsum('bhtd,bhde->bhte', qc, C) + jnp.einsum('bhts,bhse->bhte', s_qk, vc)
        nq = inter * jnp.einsum('bhtd,bhd->bht', qc, n) + jnp.sum(s_qk, axis=-1)
        h = num / jnp.maximum(jnp.abs(nq), jnp.exp(-m_t))[..., None]
        b_last = b[..., -1]
        w_log = b_last[..., None] - b + ic
        m_new = jnp.maximum(b_last + m, jnp.max(w_log, axis=-1))
        w = jnp.exp(w_log - m_new[..., None])
        decay = jnp.exp(b_last + m - m_new)
        C_new = decay[..., None, None] * C + jnp.einsum('bhs,bhsd,bhse->bhde', w, kc, vc)
        n_new = decay[..., None] * n + jnp.einsum('bhs,bhsd->bhd', w, kc)
        return (C_new, n_new, m_new), h

    init = (jnp.zeros((B, NH, D, D), jnp.float32),
            jnp.zeros((B, NH, D), jnp.float32),
            jnp.zeros((B, NH), jnp.float32))
    _, h = lax.scan(step, init, xs)
    return jnp.moveaxis(h, 0, 2).reshape(B, NH, S, D)


def token_mixer(h, w_in, w_conv, b_conv, b_igate, b_fgate, q_norm_g, k_norm_g, mlstm_norm_g, w_out):
    B, S, _ = h.shape
    xin = h @ w_in
    cuts = list(np.cumsum(IN_SPLITS)[:-1])
    qa, ka, va, qm, km, vm, og, ig, fg = jnp.split(xin, cuts, axis=-1)
    qa = rms_norm(qa.reshape(B, S, N_ATTN_HEADS, ATTN_HEAD_DIM), q_norm_g) * ATTN_SCALE
    ka = rms_norm(ka.reshape(B, S, N_ATTN_HEADS, ATTN_HEAD_DIM), k_norm_g)
    va = va.reshape(B, S, N_ATTN_HEADS, ATTN_HEAD_DIM)
    attn = dilated_attention(qa, ka, va).reshape(B, S, ATTN_WIDTH)
    qkm = jax.nn.silu(causal_short_conv(jnp.concatenate([qm, km], axis=-1), w_conv, b_conv))
    qm, km = jnp.split(qkm, 2, axis=-1)

    def heads(t):
        return t.reshape(B, S, N_MLSTM_HEADS, MLSTM_HEAD_DIM).transpose(0, 2, 1, 3).astype(jnp.float32)

    log_i = (ig + b_igate).astype(jnp.float32).transpose(0, 2, 1)
    log_f = jax.nn.log_sigmoid((fg + b_fgate).astype(jnp.float32)).transpose(0, 2, 1)
    hm = mlstm_chunkwise(heads(qm), heads(km), heads(vm), log_i, log_f)
    hm = hm.transpose(0, 2, 1, 3).astype(h.dtype)
    hm = rms_norm(hm, mlstm_norm_g.reshape(N_MLSTM_HEADS, MLSTM_HEAD_DIM)).reshape(B, S, MLSTM_WIDTH)
    hm = jax.nn.sigmoid(og) * hm
    return jnp.concatenate([attn, hm], axis=-1) @ w_out


def swiglu(h, w_gate, w_up, w_down):
    return (jax.nn.silu(h @ w_gate) * (h @ w_up)) @ w_down


def setup_inputs(seed: int = 0) -> dict:
    key = jax.random.key(seed)
    ks = jax.random.split(key, 20)
    f32 = jnp.float32
    D = D_MODEL

    def nrm(k, shape, scale):
        return jax.random.normal(k, shape, f32) * scale

    b_ada = nrm(ks[15], (DEPTH, N_MOD * D), 0.02)
    gate_cols = jnp.zeros((N_MOD, D), f32).at[2].set(1.0).at[5].set(1.0).reshape(-1)
    b_ada = b_ada + gate_cols
    f_bias = jnp.linspace(3.0, 6.0, N_MLSTM_HEADS, dtype=f32)[None, :]
    return {
        "x": nrm(ks[0], (BATCH, SEQ, D), 1.0),
        "c": nrm(ks[1], (BATCH, D), 1.0),
        "g_mix": 1.0 + nrm(ks[2], (DEPTH, D), 0.02),
        "w_in": nrm(ks[3], (DEPTH, D, IN_COLS), D ** -0.5),
        "w_conv": nrm(ks[4], (DEPTH, CONV_WIDTH, 2 * MLSTM_WIDTH), CONV_WIDTH ** -0.5),
        "b_conv": nrm(ks[5], (DEPTH, 2 * MLSTM_WIDTH), 0.02),
        "b_igate": nrm(ks[6], (DEPTH, N_MLSTM_HEADS), 0.1),
        "b_fgate": f_bias + nrm(ks[7], (DEPTH, N_MLSTM_HEADS), 0.1),
        "q_norm_g": 1.0 + nrm(ks[8], (DEPTH, ATTN_HEAD_DIM), 0.02),
        "k_norm_g": 1.0 + nrm(ks[9], (DEPTH, ATTN_HEAD_DIM), 0.02),
        "mlstm_norm_g": 1.0 + nrm(ks[10], (DEPTH, MLSTM_WIDTH), 0.02),
        "w_out": nrm(ks[11], (DEPTH, D, D), D ** -0.5),
        "g_ffn": 1.0 + nrm(ks[12], (DEPTH, D), 0.02),
        "w_gate": nrm(ks[13], (DEPTH, D, D_FF), D ** -0.5),
        "w_up": nrm(ks[14], (DEPTH, D, D_FF), D ** -0.5),
        "w_down": nrm(ks[16], (DEPTH, D_FF, D), D_FF ** -0.5),
        "w_ada": nrm(ks[17], (DEPTH, D, N_MOD * D), 0.1 * D ** -0.5),
        "b_ada": b_ada,
    }


def reference(x, c, g_mix, w_in, w_conv, b_conv, b_igate, b_fgate, q_norm_g, k_norm_g,
              mlstm_norm_g, w_out, g_ffn, w_gate, w_up, w_down, w_ada, b_ada):
    for l in range(DEPTH):
        mod = jax.nn.silu(c) @ w_ada[l] + b_ada[l]
        sh_m, sc_m, gt_m, sh_f, sc_f, gt_f = jnp.split(mod, N_MOD, axis=-1)
        h = modulate(rms_norm(x, g_mix[l]), sh_m, sc_m)
        y = token_mixer(h, w_in[l], w_conv[l], b_conv[l], b_igate[l], b_fgate[l],
                        q_norm_g[l], k_norm_g[l], mlstm_norm_g[l], w_out[l])
        x = x + gt_m[:, None, :] * y
        h = modulate(rms_norm(x, g_ffn[l]), sh_f, sc_f)
        x = x + gt_f[:, None, :] * swiglu(h, w_gate[l], w_up[l], w_down[l])
    return x
```

```python
import os
import math
from contextlib import ExitStack
import numpy as np
import ml_dtypes
import concourse.bass as bass
import concourse.mybir as mybir
from concourse.bass_utils import run_bass_kernel_spmd

F32 = mybir.dt.float32
BF16 = mybir.dt.bfloat16
AF = mybir.ActivationFunctionType
ALU = mybir.AluOpType
AX = mybir.AxisListType

NCORES = 8
T = 2048
D = 1024
EPS = 1e-6
DFF = 2816
LN_KS = math.log(128 ** -0.5)


class Prog:
    ENG = ("pe", "act", "dve", "pool", "sp")

    def __init__(self, nc, stack):
        self.nc = nc
        self.NSLOT = 8
        self.sem_sets = [{e: stack.enter_context(nc.semaphore(f"s{k}_{e}")) for e in self.ENG} for k in range(2)]
        self.dsem_sets = [{q: [stack.enter_context(nc.semaphore(f"d{k}_{q}{i}")) for i in range(self.NSLOT)]
                           for q in ("sp", "pool")} for k in range(2)]
        self.cur = 0
        self.sem = self.sem_sets[0]
        self.dsem = self.dsem_sets[0]
        self.cnt = {e: 0 for e in self.ENG}
        self.dcnt = {q: 0 for q in ("sp", "pool")}
        self.ops = {e: [] for e in self.ENG}
        self.seen = {e: {} for e in self.ENG}
        self.lastw = {}
        self.readers = {}
        self.extra_tokens = []

    def _wait(self, e, tok):
        name, sem, val = tok
        if e == "pe" and name == "c_pe":
            return
        if self.seen[e].get(name, 0) >= val:
            return
        self.seen[e][name] = val
        self.ops[e].append(lambda eng, sem=sem, val=val: eng.wait_ge(sem, val))

    def _deps(self, e, reads, writes):
        for k in reads:
            t = self.lastw.get(k)
            if t is not None:
                self._wait(e, t)
        own = f"c_{e}"
        for k in writes:
            t = self.lastw.get(k)
            if t is not None and t[0] != own:
                self._wait(e, t)
            for t in self.readers.get(k, {}).values():
                if t[0] != own:
                    self._wait(e, t)

    def _commit(self, tok, reads, writes):
        for k in reads:
            r = self.readers.setdefault(k, {})
            o = r.get(tok[0])
            if o is None or o[2] < tok[2]:
                r[tok[0]] = tok
        for k in writes:
            self.lastw[k] = tok
            self.readers[k] = {}

    def op(self, e, fn, reads=(), writes=()):
        self._deps(e, reads, writes)
        self.cnt[e] += 1
        sem = self.sem[e]
        tok = (f"c_{e}", sem, self.cnt[e])
        self.ops[e].append(lambda eng, fn=fn, sem=sem: fn(eng).then_inc(sem, 1))
        self._commit(tok, reads, writes)
        return tok

    def dma(self, q, out, in_, reads=(), writes=(), **kw):
        n = self.dcnt[q]
        slot = n % self.NSLOT
        rnd = n // self.NSLOT
        sem = self.dsem[q][slot]
        name = f"d_{q}{slot}"
        if rnd > 0:
            self._wait(q, (name, sem, 16 * rnd))
        self._deps(q, reads, writes)
        self.dcnt[q] += 1
        tok = (name, sem, 16 * (rnd + 1))
        self.ops[q].append(lambda eng, out=out, in_=in_, sem=sem, kw=kw:
                           eng.dma_start(out=out, in_=in_, **kw).then_inc(sem, 16))
        self._commit(tok, reads, writes)
        return tok

    def custom(self, e, fn, tok_sem_name, sem, val, reads=(), writes=()):
        self._deps(e, reads, writes)
        tok = (tok_sem_name, sem, val)
        self.ops[e].append(fn)
        self._commit(tok, reads, writes)
        return tok

    def emit(self, name):
        nc = self.nc
        for q in ("sp", "pool"):
            n = self.dcnt[q]
            for slot in range(self.NSLOT):
                uses = (n - slot + self.NSLOT - 1) // self.NSLOT if n > slot else 0
                if uses > 0:
                    self._wait("sp", (f"d_{q}{slot}", self.dsem[q][slot], 16 * uses))
        for e in ("pe", "act", "dve", "pool"):
            if self.cnt[e] > 0:
                self._wait("sp", (f"c_{e}", self.sem[e], self.cnt[e]))
        for t in self.extra_tokens:
            self._wait("sp", t)
        if os.environ.get("KDEBUG"):
            print("stage", name, "counts", self.cnt, self.dcnt)
        ops = self.ops
        other = 1 - self.cur
        clr = [self.sem_sets[other][e] for e in self.ENG]
        for q in ("sp", "pool"):
            clr += self.dsem_sets[other][q]
        ops["sp"] = [(lambda eng, sm=sm: eng.sem_clear(sm)) for sm in clr] + ops["sp"]
        with nc.Block(name) as block:
            @block.tensor
            def _(eng):
                for f in ops["pe"]:
                    f(eng)

            @block.scalar
            def _(eng):
                for f in ops["act"]:
                    f(eng)

            @block.vector
            def _(eng):
                for f in ops["dve"]:
                    f(eng)

            @block.gpsimd
            def _(eng):
                for f in ops["pool"]:
                    f(eng)

            @block.sync
            def _(eng):
                for f in ops["sp"]:
                    f(eng)
        self.ops = {e: [] for e in self.ENG}
        self.lastw = {}
        self.readers = {}
        self.cur = other
        self.sem = self.sem_sets[other]
        self.dsem = self.dsem_sets[other]
        self.cnt = {e: 0 for e in self.ENG}
        self.dcnt = {q: 0 for q in ("sp", "pool")}
        self.seen = {e: {} for e in self.ENG}
        self.extra_tokens = []


def build(nstage=99, dbg=()):
    nc = bass.Bass("TRN2", target_bir_lowering=False)

    def din(name, shape, dt=F32):
        return nc.dram_tensor(name, list(shape), dt, kind="ExternalInput").ap()

    x_ext = din("x_ext", [4 * T, D])
    c_col = din("c_col", [128, 8])
    w_ada = din("w_ada", [D, 6 * D])
    b_ada = din("b_ada", [1, 6 * D])
    gmix_col = din("gmix_col", [128, 8])
    gffn_col = din("gffn_col", [128, 8])
    w_in = din("w_in", [D, 3592])
    wconv_col = din("wconv_col", [128, 8, 4])
    bconv_col = din("bconv_col", [128, 8])
    bgate_bc = din("bgate_bc", [128, 8])
    gq_col = din("gq_col", [128, 1])
    gk_col = din("gk_col", [128, 1])
    gml_bc = din("gml_bc", [128, 512])
    w_out = din("w_out", [D, D])
    w_gate = din("w_gate", [D, DFF])
    w_up = din("w_up", [D, DFF])
    w_down = din("w_down", [DFF, D])
    ident_d = din("ident", [128, 128], BF16)
    masks_d = din("masks", [128, 12, 128], BF16)
    bd_d = din("bd", [128, 128], BF16)
    trineg_d = din("trineg", [128, 256])
    valid_d = din("valid", [128, 1])
    bvalid_d = din("bvalid", [128, 3])
    out = nc.dram_tensor("out", [T, D], F32, kind="ExternalOutput").ap()
    dbg_out = {}
    for nm, shp, dt_ in dbg:
        dbg_out[nm] = nc.dram_tensor("dbg_" + nm, list(shp), dt_, kind="ExternalOutput").ap()
    modrow_d = nc.dram_tensor("modrow_d", [1, 6 * D], F32, kind="Internal").ap()

    with ExitStack() as top:
        P = Prog(nc, top)

        def sbt(stack, name, shape, dt):
            return stack.enter_context(nc.sbuf_tensor(name, list(shape), dt))

        def pst(stack, name, shape, dt):
            return stack.enter_context(nc.psum_tensor(name, list(shape), dt))

        ident = sbt(top, "ident_s", [128, 128], BF16)
        masks = sbt(top, "masks_s", [128, 12, 128], BF16)
        bd = sbt(top, "bd_s", [128, 128], BF16)
        trineg = sbt(top, "trineg_s", [128, 256], F32)
        valid = sbt(top, "valid_s", [128, 1], F32)
        modc = sbt(top, "modc", [128, 48], F32)
        am = sbt(top, "am", [128, 8], F32)
        af_ = sbt(top, "af", [128, 8], F32)
        gq = sbt(top, "gq", [128, 1], F32)
        gk = sbt(top, "gk", [128, 1], F32)
        attnT = sbt(top, "attnT", [128, 4, T], BF16)
        hmT = sbt(top, "hmT", [128, 4, T], BF16)
        epsc = sbt(top, "epsc", [128, 1], F32)

        def dump(nm, src_ap, key):
            if nm in dbg_out:
                P.dma("sp", dbg_out[nm], src_ap, reads=[key])

        cast_i = [0]

        def load_w(stack_stage, dst_fn, src_fn, nparts, width, keyfn, stg):
            for i in range(nparts):
                s = stg[cast_i[0] % len(stg)]
                skey = ("stg", s.name)
                P.dma("sp", s[:, 0:width], src_fn(i), writes=[skey])
                e = ("act", "dve")[cast_i[0] % 2]
                dst = dst_fn(i)
                if e in ("pool", "dve"):
                    P.op(e, lambda eng, dst=dst, s=s: eng.tensor_copy(out=dst, in_=s[:, 0:width]),
                         reads=[skey], writes=[keyfn(i)])
                else:
                    P.op("act", lambda eng, dst=dst, s=s: eng.activation(out=dst, in_=s[:, 0:width], func=AF.Copy),
                         reads=[skey], writes=[keyfn(i)])
                cast_i[0] += 1

        def make_hT_A(src_rows_ap, bufs, idx):
            nx = len(bufs["xs"])
            xt = bufs["xt"][idx % 2]
            xs = bufs["xs"][idx % nx]
            st = bufs["st"][idx % nx]
            kx, ks, kst = ("xt", idx % 2), ("xs", idx % nx), ("st", idx % nx)
            if src_rows_ap is not None:
                P.dma("sp", xt[:, :], src_rows_ap, writes=[kx])
            P.op("act", lambda eng: eng.activation(out=xs[:, :], in_=xt[:, :], func=AF.Square,
                                                    accum_out=st[:, 0:1]), reads=[kx], writes=[ks, kst])
            P.op("act", lambda eng: eng.activation(out=st[:, 1:2], in_=st[:, 0:1], func=AF.Ln,
                                                    scale=1.0 / D, bias=epsc[:, 0:1]), reads=[kst], writes=[kst])
            P.op("act", lambda eng: eng.activation(out=st[:, 2:3], in_=st[:, 1:2], func=AF.Exp, scale=-0.5),
                 reads=[kst], writes=[kst])
            P.op("dve", lambda eng: eng.tensor_scalar(out=xs[:, :], in0=xt[:, :], scalar1=st[:, 2:3], scalar2=None,
                                                      op0=ALU.mult), reads=[kx, kst], writes=[ks])

        def make_hT_B(a_col, sh_col, dst_fn, dst_key, bufs, idx):
            nx = len(bufs["xs"])
            xs = bufs["xs"][idx % nx]
            psT = bufs["psT"][idx % 2]
            ks, kp = ("xs", idx % nx), ("psT", idx % 2)
            for k in range(8):
                P.op("pe", lambda eng, k=k: eng.transpose(out=psT[:, k, :], in_=xs[:, k * 128:(k + 1) * 128],
                                                          identity=ident[:, :]), reads=[ks], writes=[kp])
            for k in range(8):
                P.op("dve", lambda eng, k=k: eng.tensor_scalar(out=dst_fn(k), in0=psT[:, k, :],
                                                               scalar1=a_col[:, k:k + 1], scalar2=sh_col[:, k:k + 1],
                                                               op0=ALU.mult, op1=ALU.add),
                     reads=[kp], writes=[(dst_key, k)])

        def make_hT(src_rows_ap, a_col, sh_col, dst_fn, dst_key, bufs, idx, x1_dst=None):
            make_hT_A(src_rows_ap, bufs, idx)
            make_hT_B(a_col, sh_col, dst_fn, dst_key, bufs, idx)

        with ExitStack() as st0:
            wst = [sbt(st0, f"wada{i}", [128, 8, 512], F32) for i in range(4)]
            ccol = sbt(st0, "ccol", [128, 8], F32)
            scol = sbt(st0, "scol", [128, 8], F32)
            brow = sbt(st0, "brow", [1, 6 * D], F32)
            mrow = sbt(st0, "mrow", [1, 6 * D], F32)
            gmc = sbt(st0, "gmc", [128, 8], F32)
            gfc = sbt(st0, "gfc", [128, 8], F32)
            onesr = sbt(st0, "onesr", [1, 128], F32)
            noncet = sbt(st0, "noncet", [128, 1], F32)
            ps0 = pst(st0, "ps0", [128, 6, 512], F32)
            psc = pst(st0, "psc", [128, 512], F32)
            for dst, src, key in ((ident[:, :], ident_d, "ident"), (masks[:, :, :], masks_d, "masks"),
                                  (bd[:, :], bd_d, "bd"), (trineg[:, :], trineg_d, "trineg"),
                                  (valid[:, :], valid_d, "valid"), (ccol[:, :], c_col, "ccol"),
                                  (brow[:, :], b_ada, "brow"), (gmc[:, :], gmix_col, "gmc"),
                                  (gfc[:, :], gffn_col, "gfc"), (gq[:, :], gq_col, "gq"), (gk[:, :], gk_col, "gk")):
                P.dma("sp", dst, src, writes=[key])
            P.op("dve", lambda eng: eng.memset(epsc[:, :], EPS), writes=["epsc"])
            nonce_val = float(int.from_bytes(os.urandom(3), "little"))
            P.op("dve", lambda eng: eng.memset(noncet[:, :], nonce_val), writes=["noncet"])
            P.op("dve", lambda eng: eng.memset(onesr[:, :], 1.0), writes=["onesr"])
            P.op("act", lambda eng: eng.activation(out=scol[:, :], in_=ccol[:, :], func=AF.Silu),
                 reads=["ccol"], writes=["scol"])
            P.op("dve", lambda eng: eng.tensor_scalar(out=gq[:, :], in0=gq[:, :], scalar1=0.125, scalar2=None,
                                                      op0=ALU.mult), reads=["gq"], writes=["gq"])
            w_ada_v = w_ada.rearrange("(k p) c -> p k c", p=128)
            for n in range(12):
                w = wst[n % 4]
                P.dma("sp", w[:, :, :], w_ada_v[:, :, n * 512:(n + 1) * 512], writes=[("wada", n % 4)])
                hb = n % 6
                for k in range(8):
                    P.op("pe", lambda eng, w=w, k=k, hb=hb: eng.matmul(out=ps0[0:1, hb, :], lhsT=scol[:, k:k + 1],
                                                                       rhs=w[:, k, :], start=(k == 0), stop=(k == 7)),
                         reads=[("wada", n % 4), "scol"], writes=[("ps0", hb)])
                P.op("dve", lambda eng, n=n, hb=hb: eng.tensor_tensor(out=mrow[0:1, n * 512:(n + 1) * 512],
                                                                      in0=ps0[0:1, hb, :],
                                                                      in1=brow[0:1, n * 512:(n + 1) * 512], op=ALU.add),
                     reads=[("ps0", hb), "brow"], writes=["mrow"])
            for cc in range(48):
                P.op("pe", lambda eng, cc=cc: eng.matmul(out=psc[:, cc:cc + 1], lhsT=mrow[0:1, cc * 128:(cc + 1) * 128],
                                                         rhs=onesr[0:1, 0:1], start=True, stop=True),
                     reads=["mrow", "onesr"], writes=["psc"])
            P.op("dve", lambda eng: eng.tensor_copy(out=modc[:, :], in_=psc[:, 0:48]), reads=["psc"], writes=["modc"])
            P.dma("sp", modrow_d, mrow[0:1, :], reads=["mrow"], writes=["modrow_d"])
            P.op("dve", lambda eng: eng.scalar_tensor_tensor(out=am[:, :], in0=modc[:, 8:16], scalar=1.0, in1=gmc[:, :],
                                                             op0=ALU.add, op1=ALU.mult),
                 reads=["modc", "gmc"], writes=["am"])
            P.op("dve", lambda eng: eng.scalar_tensor_tensor(out=af_[:, :], in0=modc[:, 32:40], scalar=1.0, in1=gfc[:, :],
                                                             op0=ALU.add, op1=ALU.mult),
                 reads=["modc", "gfc"], writes=["af"])
            dump("modc", modc[:, :], "modc")
            P.emit("st0")
        shm = modc[:, 0:8]
        shf = modc[:, 24:32]
        if nstage <= 0:
            return nc

        with ExitStack() as sA:
            qT = sbt(sA, "qT", [128, 4, T], BF16)
            kT = sbt(sA, "kT", [128, 4, 2 * T], BF16)
            VT = sbt(sA, "VT", [128, 4, 2 * T], BF16)
            with ExitStack() as s1:
                wA = sbt(s1, "wA", [128, 8, 1536], BF16)
                stg = [sbt(s1, f"stgA{i}", [128, 1536], F32) for i in range(4)]
                bufs = dict(xt=[sbt(s1, f"xt{i}", [128, D], F32) for i in range(2)],
                            xs=[sbt(s1, f"xs{i}", [128, D], BF16) for i in range(2)],
                            st=[sbt(s1, f"st{i}", [128, 4], F32) for i in range(2)],
                            psT=[pst(s1, f"psT{i}", [128, 8, 128], BF16) for i in range(2)])
                hTg = [sbt(s1, f"hTg{i}", [128, 8, 512], BF16) for i in range(2)]
                sq = [sbt(s1, f"sq{i}", [128, 512], BF16) for i in range(2)]
                rt = [sbt(s1, f"rt{i}", [128, 512], F32) for i in range(2)]
                psq = [pst(s1, f"psq{i}", [128, 512], F32) for i in range(4)]
                pss = [pst(s1, f"pss{i}", [128, 512], F32) for i in range(1)] * 2
                load_w(s1, lambda i: wA[:, i, :], lambda i: w_in[i * 128:(i + 1) * 128, 0:1536], 8, 1536,
                       lambda i: ("wA", i), stg)
                ti = 0
                ci = 0
                pendA = []
                for g in range(8):
                    hg = hTg[g % 2]
                    hkey = ("hTg", g % 2)
                    for j in range(4):
                        r0 = 2 * T + g * 512 + j * 128
                        make_hT(x_ext[r0:r0 + 128, :], am, shm,
                                lambda k, hg=hg, j=j: hg[:, k, j * 128:(j + 1) * 128], hkey, bufs, ti)
                        ti += 1
                    chunks = range(4, 12) if g < 4 else range(12)
                    for ch in chunks:
                        ps = psq[ci % 4]
                        pk = ("psq", ci % 4)
                        for k in range(8):
                            P.op("pe", lambda eng, ps=ps, k=k, ch=ch, hg=hg: eng.matmul(
                                out=ps[:, :], lhsT=wA[:, k, ch * 128:(ch + 1) * 128], rhs=hg[:, k, :],
                                start=(k == 0), stop=(k == 7)), reads=[("wA", k), (hkey, k)], writes=[pk])
                        if ch >= 8:
                            dst = VT[:, ch - 8, g * 512:(g + 1) * 512]
                            P.op("act", lambda eng, ps=ps, dst=dst: eng.activation(out=dst, in_=ps[:, :], func=AF.Copy),
                                 reads=[pk], writes=["VT"])
                        else:
                            s_ = sq[ci % 2]
                            r_ = rt[ci % 2]
                            p2 = pss[ci % 2]
                            ksq, krt, kp2 = ("sq", ci % 2), ("rt", ci % 2), ("pss", 0)
                            P.op("act", lambda eng, ps=ps, s_=s_: eng.activation(out=s_[:, :], in_=ps[:, :], func=AF.Square),
                                 reads=[pk], writes=[ksq])
                            if ch < 4:
                                dst = qT[:, ch, (g - 4) * 512:(g - 3) * 512]
                                gcol = gq
                                dk = "qT"
                            else:
                                dst = kT[:, ch - 4, g * 512:(g + 1) * 512]
                                gcol = gk
                                dk = "kT"

                            def tailA(ps=ps, s_=s_, r_=r_, p2=p2, ksq=ksq, krt=krt, kp2=kp2, pk=pk, dst=dst, gcol=gcol, dk=dk):
                                P.op("pe", lambda eng: eng.matmul(out=p2[:, :], lhsT=bd[:, :], rhs=s_[:, :], start=True, stop=True),
                                     reads=[ksq, "bd"], writes=[kp2])
                                P.op("act", lambda eng: eng.activation(out=r_[:, :], in_=p2[:, :], func=AF.Ln,
                                                                       scale=1.0 / 64, bias=epsc[:, 0:1]),
                                     reads=[kp2], writes=[krt])
                                P.op("act", lambda eng: eng.activation(out=r_[:, :], in_=r_[:, :], func=AF.Exp, scale=-0.5),
                                     reads=[krt], writes=[krt])
                                P.op("dve", lambda eng: eng.scalar_tensor_tensor(
                                    out=dst, in0=ps[:, :], scalar=gcol[:, 0:1], in1=r_[:, :], op0=ALU.mult, op1=ALU.mult),
                                     reads=[pk, krt], writes=[dk])
                            pendA.append(tailA)
                        while len(pendA) > (1 if ch < 8 else 0):
                            pendA.pop(0)()
                        ci += 1
                dump("qT", qT[:, :, :], "qT")
                dump("kT", kT[:, :, :], "kT")
                dump("VT", VT[:, :, :], "VT")
                P.emit("stA")
            if nstage <= 1:
                return nc

            with ExitStack() as s2:
                tiles = []
                for d in (1, 4, 16):
                    for r in range(d):
                        for i in range(-1, 16 // d):
                            tiles.append((d, r, i))
                vt_idx = {t: n for n, t in enumerate(tiles)}
                NVT = len(tiles)
                Vaug = sbt(s2, "Vaug", [128, 2, NVT, 128], BF16)
                accs = [sbt(s2, f"acc{i}", [128, T], F32) for i in range(2)]
                pT = [sbt(s2, f"pT{i}", [128, 4, 128], BF16) for i in range(4)]
                psV = [pst(s2, f"psV{i}", [128, 8, 128], BF16) for i in range(1)] * 2
                psS = [pst(s2, f"psS{i}", [128, 4, 128], F32) for i in range(4)]
                psO = [pst(s2, f"psO{i}", [128, 512], F32) for i in range(2)]
                psR = pst(s2, "psR", [128, 512], F32)
                P.op("pool", lambda eng: eng.memset(Vaug[:, :, :, 64:128], 1.0), writes=["Vones"])
                bi = 0
                si = 0
                for hp in range(4):
                    for b0 in range(0, NVT, 4):
                        pv = psV[0]
                        kpv = ("psV", 0)
                        nb = min(4, NVT - b0)
                        for u in range(nb):
                            d, r, i = tiles[b0 + u]
                            base = T + r + d * 128 * i
                            P.op("pe", lambda eng, pv=pv, u=u, base=base, d=d, hp=hp: eng.transpose(
                                out=pv[:, u, :], in_=VT[:, hp, base:base + 127 * d + 1:d], identity=ident[:, :]),
                                 reads=["VT"], writes=[kpv])
                        e = "dve" if bi % 2 == 0 else "act"
                        for hh in range(2):
                            dst = Vaug[:, hh, b0:b0 + nb, 0:64]
                            src = pv[:, 0:nb, hh * 64:(hh + 1) * 64]
                            if e == "dve":
                                P.op("dve", lambda eng, dst=dst, src=src: eng.tensor_copy(out=dst, in_=src),
                                     reads=[kpv], writes=[("Vaug", hh)])
                            else:
                                P.op("act", lambda eng, dst=dst, src=src: eng.activation(out=dst, in_=src, func=AF.Copy),
                                     reads=[kpv], writes=[("Vaug", hh)])
                        bi += 1
                    for hh in range(2):
                        pb = hh * 64
                        acc = accs[hh]
                        kacc = ("acc", hh)
                        batches = []
                        for d in (1, 4):
                            for r in range(d):
                                for ib in range(0, 16 // d, 2):
                                    batches.append((d, [(r, ib), (r, ib + 1)], 0 if ib == 0 else 4))
                        for r in range(0, 16, 2):
                            batches.append((16, [(r, 0), (r + 1, 0)], 8))
                        acc3 = acc[:, :].rearrange("p (l c) -> p c l", c=16)
                        def emit_S(bidx, d, units, mslot, pb=pb, hp=hp):
                            pS = psS[bidx % 4]
                            kS = ("psS", bidx % 4)
                            for u, (r, i) in enumerate(units):
                                q0 = r + d * 128 * i
                                qsl = slice(q0, q0 + 127 * d + 1, d)
                                kc = slice(T + q0, T + q0 + 127 * d + 1, d)
                                kp_ = slice(T + q0 - 128 * d, T + q0 - d + 1, d)
                                P.op("pe", lambda eng, pS=pS, kp_=kp_, qsl=qsl, u=u, pb=pb, hp=hp: eng.matmul(
                                    out=pS[:, 2 * u, :], lhsT=kT[pb:pb + 64, hp, kp_], rhs=qT[pb:pb + 64, hp, qsl],
                                    start=True, stop=True), reads=["kT", "qT"], writes=[kS])
                                P.op("pe", lambda eng, pS=pS, kc=kc, qsl=qsl, u=u, pb=pb, hp=hp: eng.matmul(
                                    out=pS[:, 2 * u + 1, :], lhsT=kT[pb:pb + 64, hp, kc], rhs=qT[pb:pb + 64, hp, qsl],
                                    start=True, stop=True), reads=["kT", "qT"], writes=[kS])

                        def emit_rest(bidx, d, units, mslot, hh=hh):
                            pS = psS[bidx % 4]
                            p_ = pT[bidx % 4]
                            pO = psO[bidx % 2]
                            kS, kP, kO = ("psS", bidx % 4), ("pT", bidx % 4), ("psO", bidx % 2)
                            P.op("act", lambda eng, pS=pS, p_=p_: eng.activation(out=p_[:, :, :], in_=pS[:, :, :], func=AF.Exp),
                                 reads=[kS], writes=[kP])
                            mk = masks[:, mslot:mslot + 4, :]
                            P.op("dve", lambda eng, p_=p_, mk=mk: eng.tensor_tensor(out=p_[:, :, :], in0=p_[:, :, :], in1=mk, op=ALU.mult),
                                 reads=[kP, "masks"], writes=[kP])
                            for u, (r, i) in enumerate(units):
                                v0 = vt_idx[(d, r, i - 1)]
                                v1 = vt_idx[(d, r, i)]
                                P.op("pe", lambda eng, pO=pO, p_=p_, v0=v0, u=u, hh=hh: eng.matmul(
                                    out=pO[:, u * 128:(u + 1) * 128], lhsT=Vaug[:, hh, v0, :], rhs=p_[:, 2 * u, :], start=True, stop=False),
                                     reads=[kP, ("Vaug", hh), "Vones"], writes=[kO])
                                P.op("pe", lambda eng, pO=pO, p_=p_, v1=v1, u=u, hh=hh: eng.matmul(
                                    out=pO[:, u * 128:(u + 1) * 128], lhsT=Vaug[:, hh, v1, :], rhs=p_[:, 2 * u + 1, :], start=False, stop=True),
                                     reads=[kP, ("Vaug", hh), "Vones"], writes=[kO])
                            r0_, i0_ = units[0]
                            q0 = r0_ + d * 128 * i0_
                            if d == 1:
                                P.op("dve", lambda eng, pO=pO, q0=q0, acc=acc: eng.tensor_copy(out=acc[:, q0:q0 + 256], in_=pO[:, 0:256]),
                                     reads=[kO], writes=[kacc])
                            elif d == 4:
                                asl = slice(q0, q0 + 255 * 4 + 1, 4)
                                P.op("dve", lambda eng, pO=pO, asl=asl, acc=acc: eng.tensor_tensor(
                                    out=acc[:, asl], in0=acc[:, asl], in1=pO[:, 0:256], op=ALU.add),
                                     reads=[kO, kacc], writes=[kacc])
                            else:
                                a3 = acc3[:, r0_:r0_ + 2, :]
                                P.op("dve", lambda eng, pO=pO, a3=a3: eng.tensor_tensor(
                                    out=a3, in0=a3, in1=pO[:, 0:256].rearrange("p (c l) -> p c l", c=2), op=ALU.add),
                                     reads=[kO, kacc], writes=[kacc])

                        LOOK = 3
                        nb_ = len(batches)
                        for bi_ in range(min(LOOK, nb_)):
                            emit_S(si + bi_, *batches[bi_])
                        for bi_ in range(nb_):
                            emit_rest(si + bi_, *batches[bi_])
                            if bi_ + LOOK < nb_:
                                emit_S(si + bi_ + LOOK, *batches[bi_ + LOOK])
                        si += nb_
                        for cq in range(4):
                            csl = slice(cq * 512, (cq + 1) * 512)
                            P.op("act", lambda eng, csl=csl, acc=acc: eng.activation(out=psR[64:128, :], in_=acc[64:128, csl], func=AF.Ln),
                                 reads=[kacc], writes=["psR"])
                            P.op("act", lambda eng: eng.activation(out=psR[64:128, :], in_=psR[64:128, :], func=AF.Exp, scale=-1.0),
                                 reads=["psR"], writes=["psR"])
                            P.op("dve", lambda eng, pb=pb, hp=hp, csl=csl, acc=acc: eng.tensor_tensor(
                                out=attnT[pb:pb + 64, hp, csl], in0=acc[0:64, csl], in1=psR[64:128, :], op=ALU.mult),
                                 reads=[kacc, "psR"], writes=["attnT"])
                dump("attnT", attnT[:, :, :], "attnT")
                P.emit("stATT")
        if nstage <= 2:
            return nc

        with ExitStack() as sM:
            qmT = sbt(sM, "qmT", [128, 4, T], BF16)
            kmT = sbt(sM, "kmT", [128, 4, T], BF16)
            vaug = sbt(sM, "vaug", [128, 16, 4, 130], BF16)
            kw = sbt(sM, "kw", [128, 16, 4, 128], BF16)
            sog = sbt(sM, "sog", [128, 16, 512], BF16)
            gts = sbt(sM, "gts", [128, 16, 8], F32)
            gr = sbt(sM, "gr", [128, 16, 4], F32)
            gwp = sbt(sM, "gwp", [128, 16, 4], F32)
            gwpp = sbt(sM, "gwpp", [128, 16, 4], F32)
            gdec = sbt(sM, "gdec", [128, 16, 4], F32)
            Cst = sbt(sM, "Cst", [128, 4, 130], F32)
            Cbf = sbt(sM, "Cbf", [128, 4, 130], BF16)
            gml = sbt(sM, "gml", [128, 512], F32)
            with ExitStack() as s3:
                wM = sbt(s3, "wM", [128, 8, 2056], BF16)
                stg = [sbt(s3, f"stgM{i}", [128, 514], F32) for i in range(2)]
                bufs = dict(xt=[sbt(s3, f"xtm{i}", [128, D], F32) for i in range(2)],
                            xs=[sbt(s3, f"xsm{i}", [128, D], BF16) for i in range(4)],
                            st=[sbt(s3, f"stm{i}", [128, 4], F32) for i in range(4)],
                            psT=[pst(s3, f"psTm{i}", [128, 8, 128], BF16) for i in range(2)])
                hTgm = sbt(s3, "hTgm", [128, 8, 512], BF16)
                xraw = sbt(s3, "xraw", [128, 8, 516], BF16)
                diag = sbt(s3, "diag", [128, 8, 4, 128], BF16)
                wcv = sbt(s3, "wcv", [128, 8, 4], F32)
                bcv = sbt(s3, "bcv", [128, 8], F32)
                bgt = sbt(s3, "bgt", [128, 8], F32)
                bval = sbt(s3, "bval", [128, 3], F32)
                identf = sbt(s3, "identf", [128, 128], F32)
                kmtmp = sbt(s3, "kmtmp", [128, 4, 512], BF16)
                vtmp = vaug[:, 0:4, :, :]
                kwtmp = [sbt(s3, f"kwtmp{i}", [128, 128], BF16) for i in range(2)]
                gtmp = sbt(s3, "gtmp", [128, 4, 8], F32)
                grt = sbt(s3, "grt", [128, 4, 4], F32)
                gwpt = sbt(s3, "gwpt", [128, 4, 4], F32)
                gwppt = sbt(s3, "gwppt", [128, 4, 4], F32)
                gdect = sbt(s3, "gdect", [128, 4, 4], F32)
                lf = sbt(s3, "lf", [128, 4, 4], F32)
                bb = sbt(s3, "bb", [128, 4, 8], F32)
                t1 = sbt(s3, "t1", [128, 4, 4], F32)
                psA = [pst(s3, f"psA{i}", [128, 512], F32) for i in range(3)]
                psg = pst(s3, "psg", [128, 512], F32)
                psK = [pst(s3, f"psK{i}", [128, 1024], BF16) for i in range(2)]
                for dst, src, key in ((wcv[:, :, :], wconv_col, "wcv"), (bcv[:, :], bconv_col, "bcv"),
                                      (bgt[:, :], bgate_bc, "bgt"), (gml[:, :], gml_bc, "gml"), (bval[:, :], bvalid_d, "bval")):
                    P.dma("sp", dst, src, writes=[key])
                P.op("dve", lambda eng: eng.tensor_copy(out=identf[:, :], in_=ident[:, :]), writes=["identf"])
                for ch in range(8):
                    for j in range(4):
                        P.op("dve", lambda eng, ch=ch, j=j: eng.tensor_scalar(
                            out=diag[:, ch, j, :], in0=identf[:, :], scalar1=wcv[:, ch, j:j + 1], scalar2=None, op0=ALU.mult),
                             reads=["identf", "wcv"], writes=["diag"])
                P.op("pool", lambda eng: eng.memset(vaug[:, :, :, 128:130], 1.0), writes=["vaug"])
                P.op("pool", lambda eng: eng.memset(Cst[:, :, :], 0.0), writes=[("Cst", h_) for h_ in range(4)])
                P.op("pool", lambda eng: eng.memset(xraw[:, :, 0:4], 0.0), writes=[("xraw", c_) for c_ in range(8)])
                load_w(s3, lambda i: wM[:, i // 4, (i % 4) * 514:(i % 4 + 1) * 514],
                       lambda i: w_in[(i // 4) * 128:(i // 4 + 1) * 128, 1536 + (i % 4) * 514:1536 + (i % 4 + 1) * 514], 32, 514,
                       lambda i: ("wM", i // 4), stg)
                ks = 128 ** -0.5
                ti = 0
                ai_box = [0]
                pendM = []
                pendS = []
                kwi = 0
                kwi2 = 0
                NPG = int(os.environ.get("MA_PREFIX_GROUPS", "12"))
                for g in range(12 - NPG, 16):
                    main = g >= 12
                    hg = hTgm
                    hkey = "hTgm"
                    if g == 12 - NPG:
                        for j in range(4):
                            make_hT_A(x_ext[g * 512 + j * 128:g * 512 + (j + 1) * 128, :], bufs, g * 4 + j)
                    for j in range(4):
                        make_hT_B(am, shm, lambda k, j=j: hg[:, k, j * 128:(j + 1) * 128], hkey, bufs, g * 4 + j)
                    chs = list(range(8) if g >= 11 else range(4, 8))
                    for cidx, ch in enumerate(chs):
                        if cidx < 4 and g + 1 < 16:
                            r0n = (g + 1) * 512 + cidx * 128
                            make_hT_A(x_ext[r0n:r0n + 128, :], bufs, (g + 1) * 4 + cidx)
                        ps = psA[ai_box[0] % 3]
                        pk = ("psA", ai_box[0] % 3)
                        ai_box[0] += 1
                        for k in range(8):
                            P.op("pe", lambda eng, ps=ps, k=k, ch=ch: eng.matmul(
                                out=ps[:, :], lhsT=wM[:, k, ch * 128:(ch + 1) * 128], rhs=hg[:, k, :],
                                start=(k == 0), stop=(k == 7)), reads=[("wM", k), (hkey, k)], writes=[pk])
                        P.op("dve", lambda eng, ps=ps, ch=ch: eng.tensor_copy(out=xraw[:, ch, 3:515], in_=ps[:, :]),
                             reads=[pk], writes=[("xraw", ch)])
                        def tailM(ch=ch, g=g, main=main):
                            nonlocal_ai = ai_box
                            if main or ch >= 4:
                                ps2 = psA[nonlocal_ai[0] % 3]
                                pk2 = ("psA", nonlocal_ai[0] % 3)
                                nonlocal_ai[0] += 1
                                for j in range(4):
                                    P.op("pe", lambda eng, ps2=ps2, ch=ch, j=j: eng.matmul(
                                        out=ps2[:, :], lhsT=diag[:, ch, j, :], rhs=xraw[:, ch, j:j + 512],
                                        start=(j == 0), stop=(j == 3)), reads=["diag", ("xraw", ch)], writes=[pk2])
                                if main:
                                    dstT = qmT if ch < 4 else kmT
                                    dst = dstT[:, ch % 4, (g - 12) * 512:(g - 11) * 512]
                                    dkey = "qmT" if ch < 4 else "kmT"
                                else:
                                    dst = kmtmp[:, ch % 4, :]
                                    dkey = "kmtmp"
                                P.op("act", lambda eng, ps2=ps2, dst=dst, ch=ch: eng.activation(
                                    out=dst, in_=ps2[:, :], func=AF.Silu, bias=bcv[:, ch:ch + 1]),
                                     reads=[pk2, "bcv"], writes=[dkey])
                            if g % 4 == 3 and g < 12:
                                P.op("dve", lambda eng, ch=ch, g=g: eng.tensor_scalar(
                                    out=xraw[:, ch, 0:3], in0=xraw[:, ch, 512:515], scalar1=bval[:, g // 4:g // 4 + 1],
                                    scalar2=None, op0=ALU.mult), reads=[("xraw", ch), "bval"], writes=[("xraw", ch)])
                            else:
                                P.op("pool", lambda eng, ch=ch: eng.tensor_copy(out=xraw[:, ch, 0:3], in_=xraw[:, ch, 512:515]),
                                     reads=[("xraw", ch)], writes=[("xraw", ch)])
                        pendM.append(tailM)
                        while len(pendM) > 1:
                            pendM.pop(0)()
                    while pendM:
                        pendM.pop(0)()
                    for j in range(4):
                        tl = (g - 12) * 4 + j
                        hsl = slice(j * 128, (j + 1) * 128)
                        ps = psA[ai_box[0] % 3]
                        pk = ("psA", ai_box[0] % 3)
                        ai_box[0] += 1
                        for k in range(8):
                            P.op("pe", lambda eng, ps=ps, k=k, hsl=hsl: eng.matmul(
                                out=ps[:, :], lhsT=hg[:, k, hsl], rhs=wM[:, k, 1024:1536],
                                start=(k == 0), stop=(k == 7)), reads=[("wM", k), (hkey, k)], writes=[pk])
                        vdst = vaug[:, tl, :, 0:128] if main else vtmp[:, j, :, 0:128]
                        P.op("dve", lambda eng, ps=ps, vdst=vdst: eng.tensor_copy(
                            out=vdst, in_=ps[:, :].rearrange("p (h e) -> p h e", h=4)),
                             reads=[pk], writes=["vaug"])
                        if main:
                            ps = psA[ai_box[0] % 3]
                            pk = ("psA", ai_box[0] % 3)
                            ai_box[0] += 1
                            for k in range(8):
                                P.op("pe", lambda eng, ps=ps, k=k, hsl=hsl: eng.matmul(
                                    out=ps[:, :], lhsT=hg[:, k, hsl], rhs=wM[:, k, 1536:2048],
                                    start=(k == 0), stop=(k == 7)), reads=[("wM", k), (hkey, k)], writes=[pk])
                            P.op("act", lambda eng, ps=ps, tl=tl: eng.activation(out=sog[:, tl, :], in_=ps[:, :], func=AF.Sigmoid),
                                 reads=[pk], writes=["sog"])
                        for k in range(8):
                            P.op("pe", lambda eng, k=k, hsl=hsl: eng.matmul(
                                out=psg[:, 0:8], lhsT=hg[:, k, hsl], rhs=wM[:, k, 2048:2056],
                                start=(k == 0), stop=(k == 7)), reads=[("wM", k), (hkey, k)], writes=["psg"])
                        gdst = gts[:, tl, :] if main else gtmp[:, j, :]
                        P.op("dve", lambda eng, gdst=gdst: eng.tensor_tensor(out=gdst, in0=psg[:, 0:8], in1=bgt[:, :], op=ALU.add),
                             reads=["psg", "bgt"], writes=["gts" if main else "gtmp"])
                    if main:
                        t0_ = (g - 12) * 4
                        G_, GR, GWP, GWPP, GDEC = (gts[:, t0_:t0_ + 4, :], gr[:, t0_:t0_ + 4, :], gwp[:, t0_:t0_ + 4, :],
                                                   gwpp[:, t0_:t0_ + 4, :], gdec[:, t0_:t0_ + 4, :])
                        gk_ = "gts"
                    else:
                        G_, GR, GWP, GWPP, GDEC = gtmp[:, :, :], grt[:, :, :], gwpt[:, :, :], gwppt[:, :, :], gdect[:, :, :]
                        gk_ = "gtmp"
                    P.op("act", lambda eng, G_=G_: eng.activation(out=lf[:, :, :], in_=G_[:, :, 4:8], func=AF.Exp, scale=-1.0),
                         reads=[gk_], writes=["lf"])
                    P.op("act", lambda eng: eng.activation(out=lf[:, :, :], in_=lf[:, :, :], func=AF.Ln, bias=1.0),
                         reads=["lf"], writes=["lf"])
                    for j in range(4):
                        P.op("pe", lambda eng, j=j: eng.matmul(out=psg[:, 16 + j * 8:16 + j * 8 + 4], lhsT=trineg[:, 0:128],
                                                               rhs=lf[:, j, :], start=True, stop=True),
                             reads=["lf", "trineg"], writes=["psg"])
                        P.op("pe", lambda eng, j=j: eng.matmul(out=psg[:, 16 + j * 8 + 4:16 + j * 8 + 8], lhsT=trineg[:, 128:256],
                                                               rhs=lf[:, j, :], start=True, stop=True),
                             reads=["lf", "trineg"], writes=["psg"])
                    P.op("dve", lambda eng: eng.tensor_copy(out=bb[:, :, :], in_=psg[:, 16:48].rearrange("p (t e) -> p t e", e=8)),
                         reads=["psg"], writes=["bb"])
                    P.op("act", lambda eng, GR=GR: eng.activation(out=GR, in_=bb[:, :, 0:4], func=AF.Exp), reads=["bb"], writes=["g_r"])
                    P.op("act", lambda eng, GDEC=GDEC: eng.activation(out=GDEC, in_=bb[:, :, 4:8], func=AF.Exp), reads=["bb"], writes=["g_dec"])
                    P.op("dve", lambda eng, G_=G_: eng.tensor_tensor(out=t1[:, :, :], in0=G_[:, :, 0:4], in1=bb[:, :, 0:4], op=ALU.subtract),
                         reads=[gk_, "bb"], writes=["t1"])
                    P.op("act", lambda eng, GWP=GWP: eng.activation(out=GWP, in_=t1[:, :, :], func=AF.Exp), reads=["t1"], writes=["g_wp"])
                    P.op("dve", lambda eng: eng.tensor_tensor(out=t1[:, :, :], in0=t1[:, :, :], in1=bb[:, :, 4:8], op=ALU.add),
                         reads=["t1", "bb", "g_wp"], writes=["t1"])
                    P.op("act", lambda eng, GWPP=GWPP: eng.activation(out=GWPP, in_=t1[:, :, :], func=AF.Exp), reads=["t1"], writes=["g_wpp"])
                    P.op("dve", lambda eng, GWP=GWP: eng.tensor_scalar(out=GWP, in0=GWP, scalar1=ks, scalar2=None, op0=ALU.mult),
                         reads=["g_wp"], writes=["g_wp"])
                    if main:
                        P.op("dve", lambda eng, GWPP=GWPP: eng.tensor_scalar(out=GWPP, in0=GWPP, scalar1=ks, scalar2=None, op0=ALU.mult),
                             reads=["g_wpp"], writes=["g_wpp"])
                    else:
                        P.op("dve", lambda eng, GWPP=GWPP, g=g: eng.tensor_scalar(out=GWPP, in0=GWPP, scalar1=bval[:, g // 4:g // 4 + 1],
                                                                                scalar2=ks, op0=ALU.mult, op1=ALU.mult),
                             reads=["g_wpp", "bval"], writes=["g_wpp"])
                    for j in range(4):
                        tl = (g - 12) * 4 + j
                        for h in range(4):
                            pk_ = psK[kwi2 % 2]
                            kpk = ("psK", kwi2 % 2)
                            kwi2 += 1
                            ksrc = kmT[:, h, tl * 128:(tl + 1) * 128] if main else kmtmp[:, h, j * 128:(j + 1) * 128]
                            P.op("pe", lambda eng, ksrc=ksrc, pk_=pk_: eng.transpose(out=pk_[:, 0:128], in_=ksrc, identity=ident[:, :]),
                                 reads=["kmT" if main else "kmtmp"], writes=[kpk])
                            if main:
                                kdst = kw[:, tl, h, :]
                                kkey = "kw"
                            else:
                                kdst = kwtmp[kwi % 2][:, :]
                                kkey = ("kwtmp", kwi % 2)
                                kwi += 1
                            P.op("act", lambda eng, kdst=kdst, GWPP=GWPP, j=j, h=h, pk_=pk_: eng.activation(
                                out=kdst, in_=pk_[:, 0:128], func=AF.Copy, scale=GWPP[:, j, h:h + 1]),
                                 reads=[kpk, "g_wpp"], writes=[kkey])
                            if not main:
                                def tailS(kdst=kdst, kkey=kkey, GDEC=GDEC, j=j, h=h):
                                    ps = psA[ai_box[0] % 3]
                                    pk = ("psA", ai_box[0] % 3)
                                    ai_box[0] += 1
                                    P.op("pe", lambda eng: eng.matmul(
                                        out=ps[:, 0:129], lhsT=kdst, rhs=vtmp[:, j, h, 0:129], start=True, stop=True),
                                         reads=[kkey, "vaug"], writes=[pk])
                                    P.op("dve", lambda eng: eng.scalar_tensor_tensor(
                                        out=Cst[:, h, 0:129], in0=Cst[:, h, 0:129], scalar=GDEC[:, j, h:h + 1], in1=ps[:, 0:129],
                                        op0=ALU.mult, op1=ALU.add), reads=[pk, "g_dec", ("Cst", h)], writes=[("Cst", h)])
                                pendS.append(tailS)
                                while len(pendS) > 1:
                                    pendS.pop(0)()
                    while pendS:
                        pendS.pop(0)()
                    if g == 11:
                        P.op("act", lambda eng: eng.activation(out=Cbf[:, :, :], in_=Cst[:, :, :], func=AF.Copy),
                             reads=[("Cst", h_) for h_ in range(4)], writes=["Cbf"])
                dump("gts", gts[:, :, :].rearrange("p t e -> p (t e)"), "gts")
                P.emit("stMA")
            if nstage <= 4:
                return nc


            with ExitStack() as s5:
                At = [sbt(s5, f"At{i}", [128, 128], BF16) for i in range(2)]
                hmt = [sbt(s5, f"hmt{i}", [128, 128], BF16) for i in range(2)]
                gs = [sbt(s5, f"gs{i}", [128, 512], F32) for i in range(2)]
                sm = [sbt(s5, f"sm{i}", [128, 8, 4], F32) for i in range(2)]
                junk = sbt(s5, "junk", [128, 128], BF16)
                psS = [pst(s5, f"psSm{i}", [128, 512], F32) for i in range(2)]
                psN = pst(s5, "psN", [128, 4, 512], F32)
                psC = pst(s5, "psCm", [128, 512], F32)
                psH = pst(s5, "psH", [128, 1024], BF16)
                n = 0
                for tl in range(int(os.environ.get("MB_TILES", "16"))):
                    tsl = slice(tl * 128, (tl + 1) * 128)
                    g_ = gs[tl % 2]
                    kg = ("gs", tl % 2)
                    s_ = sm[tl % 2]
                    ksm = ("sm", tl % 2)
                    P.op("pool", lambda eng, g_=g_, tl=tl: eng.tensor_tensor(out=g_[:, :], in0=sog[:, tl, :], in1=gml[:, :], op=ALU.mult),
                         reads=["sog", "gml"], writes=[kg])
                    for h in range(4):
                        b = n % 2
                        n += 1
                        pS, A_ = psS[b], At[b]
                        kS, kA, kN = ("psS", b), ("At", b), ("psN", h)
                        P.op("pe", lambda eng, pS=pS, h=h, tsl=tsl: eng.matmul(out=pS[:, 0:128], lhsT=kmT[:, h, tsl], rhs=qmT[:, h, tsl],
                                                                               start=True, stop=True),
                             reads=["kmT", "qmT"], writes=[kS])
                        P.op("dve", lambda eng, pS=pS, A_=A_, tl=tl, h=h: eng.scalar_tensor_tensor(
                            out=A_[:, :], in0=pS[:, 0:128], scalar=gwp[:, tl, h:h + 1], in1=masks[:, 1, :], op0=ALU.mult, op1=ALU.mult),
                             reads=[kS, "gwp", "masks"], writes=[kA])
                        P.op("pe", lambda eng, A_=A_, tl=tl, h=h: eng.matmul(out=psN[:, h, 0:129], lhsT=A_[:, :],
                                                                             rhs=vaug[:, tl, h, 0:129], start=True, stop=False),
                             reads=[kA, "vaug"], writes=[kN])
                        P.op("pe", lambda eng, tsl=tsl, h=h: eng.matmul(out=psN[:, h, 0:129], lhsT=qmT[:, h, tsl],
                                                                        rhs=Cbf[:, h, 0:129], start=False, stop=True),
                             reads=["qmT", ("Cbf", h)], writes=[kN])
                        P.op("pe", lambda eng, tl=tl, h=h: eng.matmul(out=psC[:, 0:129], lhsT=kw[:, tl, h, :],
                                                                      rhs=vaug[:, tl, h, 0:129], start=True, stop=True),
                             reads=["kw", "vaug"], writes=["psC"])
                        P.op("dve", lambda eng, tl=tl, h=h: eng.scalar_tensor_tensor(
                            out=Cst[:, h, 0:129], in0=Cst[:, h, 0:129], scalar=gdec[:, tl, h:h + 1], in1=psC[:, 0:129],
                            op0=ALU.mult, op1=ALU.add), reads=["psC", "gdec", ("Cst", h)], writes=[("Cst", h)])
                        P.op("act", lambda eng, h=h: eng.activation(out=Cbf[:, h, 0:129], in_=Cst[:, h, 0:129], func=AF.Copy),
                             reads=[("Cst", h)], writes=[("Cbf", h)])
                        P.op("act", lambda eng, s_=s_, h=h: eng.activation(out=junk[:, :], in_=psN[:, h, 0:128], func=AF.Square,
                                                                           accum_out=s_[:, 4, h:h + 1]),
                             reads=[ksm], writes=[ksm, "junk", kN])
                    kNall = [("psN", h_) for h_ in range(4)]
                    P.op("dve", lambda eng, s_=s_, tl=tl: eng.tensor_tensor(out=s_[:, 0, :], in0=psN[:, :, 128], in1=gr[:, tl, :], op=ALU.mult),
                         reads=kNall + ["gr", ksm], writes=[ksm])
                    P.op("dve", lambda eng, s_=s_: eng.tensor_scalar(out=s_[:, 1, :], in0=s_[:, 0, :], scalar1=-1.0, scalar2=None, op0=ALU.mult),
                         reads=[ksm], writes=[ksm])
                    P.op("dve", lambda eng, s_=s_: eng.scalar_tensor_tensor(out=s_[:, 1, :], in0=s_[:, 0, :], scalar=1.0, in1=s_[:, 1, :],
                                                                            op0=ALU.max, op1=ALU.max), reads=[ksm], writes=[ksm])
                    P.op("dve", lambda eng, s_=s_: eng.reciprocal(out=s_[:, 2, :], in_=s_[:, 1, :]), reads=[ksm], writes=[ksm])
                    P.op("dve", lambda eng, s_=s_, tl=tl: eng.tensor_tensor(out=s_[:, 3, :], in0=s_[:, 2, :], in1=gr[:, tl, :], op=ALU.mult),
                         reads=[ksm, "gr"], writes=[ksm])
                    P.op("dve", lambda eng, s_=s_: eng.tensor_tensor(out=s_[:, 5, :], in0=s_[:, 3, :], in1=s_[:, 3, :], op=ALU.mult),
                         reads=[ksm], writes=[ksm])
                    P.op("dve", lambda eng, s_=s_: eng.scalar_tensor_tensor(out=s_[:, 5, :], in0=s_[:, 5, :], scalar=1.0 / 128,
                                                                            in1=s_[:, 4, :], op0=ALU.mult, op1=ALU.mult),
                         reads=[ksm], writes=[ksm])
                    P.op("act", lambda eng, s_=s_: eng.activation(out=s_[:, 6, :], in_=s_[:, 5, :], func=AF.Ln, bias=epsc[:, 0:1]),
                         reads=[ksm], writes=[ksm])
                    P.op("act", lambda eng, s_=s_: eng.activation(out=s_[:, 7, :], in_=s_[:, 6, :], func=AF.Exp, scale=-0.5),
                         reads=[ksm], writes=[ksm])
                    P.op("dve", lambda eng, s_=s_: eng.tensor_tensor(out=s_[:, 7, :], in0=s_[:, 7, :], in1=s_[:, 3, :], op=ALU.mult),
                         reads=[ksm], writes=[ksm])
                    for h in range(4):
                        hm_ = hmt[h % 2]
                        kh = ("hmt", h % 2)
                        P.op("dve", lambda eng, s_=s_, hm_=hm_, g_=g_, h=h: eng.scalar_tensor_tensor(
                            out=hm_[:, :], in0=psN[:, h, 0:128], scalar=s_[:, 7, h:h + 1], in1=g_[:, h * 128:(h + 1) * 128],
                            op0=ALU.mult, op1=ALU.mult), reads=[("psN", h), ksm, kg], writes=[kh])
                        P.op("pe", lambda eng, hm_=hm_: eng.transpose(out=psH[:, 0:128], in_=hm_[:, :], identity=ident[:, :]),
                             reads=[kh], writes=["psH"])
                        P.op("act", lambda eng, h=h, tsl=tsl: eng.activation(out=hmT[:, h, tsl], in_=psH[:, 0:128], func=AF.Copy),
                             reads=["psH"], writes=["hmT"])
                dump("hmT", hmT[:, :, :], "hmT")
                P.emit("stMB")
        if nstage <= 5:
            return nc

        with ExitStack() as sF:
            hTf = sbt(sF, "hTf", [128, 8, T], BF16)
            gtbc = sbt(sF, "gtbc", [128, 2, D], F32)
            x1sb = sbt(sF, "x1sb", [128, 16, D], F32)
            with ExitStack() as s6:
                wO = sbt(s6, "wO", [128, 8, D], BF16)
                stg = [sbt(s6, f"stgO{i}", [128, D], F32) for i in range(4)]
                bufs = dict(xt=[sbt(s6, f"xtf{i}", [128, D], F32) for i in range(2)],
                            xs=[sbt(s6, f"xsf{i}", [128, D], BF16) for i in range(2)],
                            st=[sbt(s6, f"stf{i}", [128, 4], F32) for i in range(2)],
                            psT=[pst(s6, f"psTf{i}", [128, 8, 128], BF16) for i in range(2)])
                xin_ = [sbt(s6, f"xin{i}", [128, D], F32) for i in range(2)]
                psY = [pst(s6, f"psY{i}", [128, 2, 512], F32) for i in range(2)]
                grow = sbt(s6, "grow", [1, 2, D], F32)
                onesr2 = sbt(s6, "onesr2", [1, 128], F32)
                P.op("dve", lambda eng: eng.memset(onesr2[:, :], 1.0), writes=["onesr2"])
                P.dma("sp", grow[0:1, 0, :], modrow_d[0:1, 2 * D:3 * D], writes=["grow"])
                P.dma("sp", grow[0:1, 1, :], modrow_d[0:1, 5 * D:6 * D], writes=["grow"])
                for gi in range(2):
                    for hh in range(2):
                        P.op("pe", lambda eng, gi=gi, hh=hh: eng.matmul(out=psY[gi][:, hh, :], lhsT=onesr2[0:1, :],
                                                                        rhs=grow[0:1, gi, hh * 512:(hh + 1) * 512],
                                                                        start=True, stop=True),
                             reads=["grow", "onesr2"], writes=[("psY", gi)])
                    P.op("dve", lambda eng, gi=gi: eng.tensor_copy(out=gtbc[:, gi, :],
                                                                   in_=psY[gi][:, :, :].rearrange("p a b -> p (a b)")),
                         reads=[("psY", gi)], writes=["gtbc"])
                load_w(s6, lambda i: wO[:, i, :], lambda i: w_out[i * 128:(i + 1) * 128, :], 8, D, lambda i: ("wO", i), stg)
                for tl in range(16):
                    tsl = slice(tl * 128, (tl + 1) * 128)
                    b = tl % 2
                    pY = psY[b]
                    kY = ("psY", b)
                    xi = xin_[b]
                    kxi = ("xin", b)
                    P.dma("sp", xi[:, :], x_ext[3 * T + tl * 128:3 * T + (tl + 1) * 128, :], writes=[kxi])
                    for nh in range(2):
                        for c in range(8):
                            src = attnT if c < 4 else hmT
                            P.op("pe", lambda eng, pY=pY, nh=nh, c=c, src=src, tsl=tsl: eng.matmul(
                                out=pY[:, nh, :], lhsT=src[:, c % 4, tsl], rhs=wO[:, c, nh * 512:(nh + 1) * 512],
                                start=(c == 0), stop=(c == 7)), reads=[("wO", c), "attnT", "hmT"], writes=[kY])
                    xt = bufs["xt"][tl % 2]
                    kx = ("xt", tl % 2)
                    P.op("dve", lambda eng, pY=pY, xt=xt: eng.tensor_tensor(out=xt[:, :], in0=pY[:, :, :].rearrange("p a b -> p (a b)"),
                                                                            in1=gtbc[:, 0, :], op=ALU.mult),
                         reads=[kY, "gtbc"], writes=[kx])
                    P.op("pool", lambda eng, xt=xt, xi=xi, tl=tl: eng.tensor_tensor(out=xt[:, :], in0=xt[:, :], in1=xi[:, :], op=ALU.add),
                         reads=[kx, kxi], writes=[kx])
                    P.op("pool", lambda eng, xt=xt, tl=tl: eng.tensor_copy(out=x1sb[:, tl, :], in_=xt[:, :]),
                         reads=[kx], writes=[("x1sb", tl)])
                    if tl > 0:
                        psl = slice((tl - 1) * 128, tl * 128)
                        make_hT(None, af_, shf, lambda k, psl=psl: hTf[:, k, psl], "hTf", bufs, tl - 1)
                make_hT(None, af_, shf, lambda k: hTf[:, k, 15 * 128:16 * 128], "hTf", bufs, 15)
                P.emit("stF1")
            if nstage <= 6:
                return nc
            with ExitStack() as s7:
                parts = [(0, 6), (6, 6), (12, 5), (17, 5)]
                WMAX = 768
                wg = sbt(s7, "wg", [128, 8, WMAX], BF16)
                wu = sbt(s7, "wu", [128, 8, WMAX], BF16)
                wd = sbt(s7, "wd", [128, 6, D], BF16)
                stg = [sbt(s7, f"stgF{i}", [128, D], F32) for i in range(4)]
                sgt = [sbt(s7, f"sgt{i}", [128, 512], F32) for i in range(2)]
                actt = [sbt(s7, f"actt{i}", [128, WMAX], BF16) for i in range(2)]
                actT = [sbt(s7, f"actT{i}", [128, 6, 128], BF16) for i in range(2)]
                ot = [sbt(s7, f"ot{i}", [128, D], F32) for i in range(1)] * 2
                psG = [pst(s7, f"psG{i}", [128, 512], F32) for i in range(2)]
                psU = [pst(s7, f"psU{i}", [128, 512], F32) for i in range(2)]
                psTt = [pst(s7, f"psTt{i}", [128, 8, 128], BF16) for i in range(2)]
                psD = pst(s7, "psD", [128, 2, 512], F32)
                gi = 0
                tti = 0
                pend = []
                pendF = []

                def emit_down(tl, aT, kaT, NF, last):
                    tsl = slice(tl * 128, (tl + 1) * 128)
                    o_ = ot[0]
                    ko = ("ot", 0)
                    for nh in range(2):
                        for f in range(NF):
                            P.op("pe", lambda eng, nh=nh, f=f, aT=aT, NF=NF: eng.matmul(
                                out=psD[:, nh, :], lhsT=aT[:, f, :], rhs=wd[:, f, nh * 512:(nh + 1) * 512],
                                start=(f == 0), stop=(f == NF - 1)), reads=[kaT, ("wd", f)], writes=["psD"])
                    P.op("dve", lambda eng, o_=o_: eng.tensor_tensor(out=o_[:, :], in0=psD[:, :, :].rearrange("p a b -> p (a b)"),
                                                                     in1=gtbc[:, 1, :], op=ALU.mult),
                         reads=["psD", "gtbc"], writes=[ko])
                    P.op("pool", lambda eng, o_=o_, tl=tl: eng.tensor_tensor(out=x1sb[:, tl, :], in0=x1sb[:, tl, :], in1=o_[:, :], op=ALU.add),
                         reads=[ko, ("x1sb", tl)], writes=[("x1sb", tl)])
                    if last:
                        P.dma("sp", out[tsl, :], x1sb[:, tl, :], reads=[("x1sb", tl)], writes=[("out", tl)])

                nparts = int(os.environ.get("F2_PARTS", "4"))
                for pi, (f_lo, NF) in enumerate(parts[:nparts]):
                    h0 = f_lo * 128
                    HW_ = NF * 128
                    last = (pi == len(parts) - 1)
                    load_w(s7, lambda i, HW_=HW_: wg[:, i, 0:HW_], lambda i, h0=h0, HW_=HW_: w_gate[i * 128:(i + 1) * 128, h0:h0 + HW_],
                           8, HW_, lambda i: ("wg", i), stg)
                    load_w(s7, lambda i, HW_=HW_: wu[:, i, 0:HW_], lambda i, h0=h0, HW_=HW_: w_up[i * 128:(i + 1) * 128, h0:h0 + HW_],
                           8, HW_, lambda i: ("wu", i), stg)
                    load_w(s7, lambda i: wd[:, i, :], lambda i, h0=h0: w_down[h0 + i * 128:h0 + (i + 1) * 128, :], NF, D,
                           lambda i: ("wd", i), stg)
                    pieces = [(0, 512), (512, HW_ - 512)]
                    for tl in range(int(os.environ.get("F2_TILES", "16"))):
                        tsl = slice(tl * 128, (tl + 1) * 128)
                        b = tl % 2
                        a_, aT, o_ = actt[b], actT[b], ot[b]
                        ka, kaT, ko = ("actt", b), ("actT", b), ("ot", 0)
                        for (c0, w) in pieces:
                            pG, pU, sg = psG[gi % 2], psU[gi % 2], sgt[gi % 2]
                            kG, kU, ksg = ("psG", gi % 2), ("psU", gi % 2), ("sgt", gi % 2)
                            gi += 1
                            for k in range(8):
                                P.op("pe", lambda eng, pG=pG, k=k, c0=c0, w=w, tsl=tsl: eng.matmul(
                                    out=pG[:, 0:w], lhsT=hTf[:, k, tsl], rhs=wg[:, k, c0:c0 + w], start=(k == 0), stop=(k == 7)),
                                     reads=[("hTf", k), ("wg", k)], writes=[kG])
                            for k in range(8):
                                P.op("pe", lambda eng, pU=pU, k=k, c0=c0, w=w, tsl=tsl: eng.matmul(
                                    out=pU[:, 0:w], lhsT=hTf[:, k, tsl], rhs=wu[:, k, c0:c0 + w], start=(k == 0), stop=(k == 7)),
                                     reads=[("hTf", k), ("wu", k)], writes=[kU])
                            P.op("act", lambda eng, pG=pG, sg=sg, w=w: eng.activation(out=sg[:, 0:w], in_=pG[:, 0:w], func=AF.Silu),
                                 reads=[kG], writes=[ksg])
                            P.op("dve", lambda eng, pU=pU, sg=sg, a_=a_, c0=c0, w=w: eng.tensor_tensor(
                                out=a_[:, c0:c0 + w], in0=sg[:, 0:w], in1=pU[:, 0:w], op=ALU.mult),
                                 reads=[ksg, kU], writes=[ka])
                            nf = w // 128
                            pt = psTt[tti % 2]
                            kpt = ("psTt", tti % 2)
                            tti += 1
                            f0 = c0 // 128

                            def tailF(nf=nf, pt=pt, kpt=kpt, a_=a_, c0=c0, ka=ka, aT=aT, f0=f0, kaT=kaT):
                                for u in range(nf):
                                    P.op("pe", lambda eng, u=u: eng.transpose(
                                        out=pt[:, u, :], in_=a_[:, c0 + u * 128:c0 + (u + 1) * 128], identity=ident[:, :]),
                                         reads=[ka], writes=[kpt])
                                P.op("act", lambda eng: eng.activation(
                                    out=aT[:, f0:f0 + nf, :], in_=pt[:, 0:nf, :], func=AF.Copy), reads=[kpt], writes=[kaT])
                            pendF.append(tailF)
                            while len(pendF) > 2:
                                pendF.pop(0)()
                        pend.append((tl, aT, kaT, NF, last))
                        if len(pend) > 1:
                            emit_down(*pend.pop(0))
                    while pendF:
                        pendF.pop(0)()
                    while pend:
                        emit_down(*pend.pop(0))
                P.emit("stF2")
    return nc


def _consts(core):
    b, j = divmod(core, 4)
    k = np.arange(128)[:, None]
    q = np.arange(128)[None, :]
    cur = (k <= q).astype(np.float32)
    prev = (k >= q).astype(np.float32)
    pf = prev * (1.0 if j > 0 else 0.0)
    masks = np.stack([pf, cur, prev, cur, prev, cur, prev, cur, pf, cur, pf, cur], axis=1).astype(ml_dtypes.bfloat16)
    ident = np.eye(128, dtype=np.float32).astype(ml_dtypes.bfloat16)
    bd = np.kron(np.eye(2), np.ones((64, 64))).astype(ml_dtypes.bfloat16)
    trineg = np.concatenate([-(k <= q).astype(np.float32), -np.ones((128, 128), np.float32)], axis=1)
    valid = np.full((128, 1), 1.0 if j > 0 else 0.0, np.float32)
    bvalid = np.zeros((128, 3), np.float32)
    for blk in range(3):
        if j - 3 + blk >= 0:
            bvalid[:, blk] = 1.0
    return dict(ident=ident, masks=np.ascontiguousarray(masks), bd=bd, trineg=np.ascontiguousarray(trineg), valid=valid, bvalid=bvalid)


def make_in_maps(inputs):
    f = lambda a: np.ascontiguousarray(np.asarray(a, dtype=np.float32))
    x = f(inputs["x"])
    col = lambda v: np.ascontiguousarray(v.reshape(-1, 128).T)
    shared = dict(
        w_ada=f(inputs["w_ada"][0]), b_ada=f(inputs["b_ada"][0]).reshape(1, -1),
        gmix_col=col(f(inputs["g_mix"][0])), gffn_col=col(f(inputs["g_ffn"][0])),
        w_in=f(inputs["w_in"][0]),
        wconv_col=np.ascontiguousarray(f(inputs["w_conv"][0]).reshape(4, 8, 128).transpose(2, 1, 0)),
        bconv_col=col(f(inputs["b_conv"][0])),
        bgate_bc=np.ascontiguousarray(np.broadcast_to(
            np.concatenate([f(inputs["b_igate"][0]), f(inputs["b_fgate"][0])])[None, :], (128, 8))),
        gq_col=np.ascontiguousarray(np.tile(f(inputs["q_norm_g"][0]), 2).reshape(128, 1)),
        gk_col=np.ascontiguousarray(np.tile(f(inputs["k_norm_g"][0]), 2).reshape(128, 1)),
        gml_bc=np.ascontiguousarray(np.broadcast_to(f(inputs["mlstm_norm_g"][0])[None, :], (128, 512))),
        w_out=f(inputs["w_out"][0]), w_gate=f(inputs["w_gate"][0]), w_up=f(inputs["w_up"][0]),
        w_down=f(inputs["w_down"][0]),
    )
    maps = []
    for core in range(NCORES):
        b, j = divmod(core, 4)
        xe = np.zeros((4 * T, D), np.float32)
        xe[3 * T:] = x[b, j * T:(j + 1) * T]
        if j > 0:
            xe[3 * T - j * T:3 * T] = x[b, 0:j * T]
        m = dict(shared)
        m["x_ext"] = xe
        m["c_col"] = col(f(inputs["c"][b]))
        m.update(_consts(core))
        maps.append(m)
    return maps


def kernel(**inputs):
    nc = build()
    maps = make_in_maps(inputs)
    res = run_bass_kernel_spmd(nc, maps, core_ids=list(range(NCORES)))
    outp = np.zeros((2, 4 * T, D), np.float32)
    for core in range(NCORES):
        b, j = divmod(core, 4)
        outp[b, j * T:(j + 1) * T] = np.asarray(res.results[core]["out"], dtype=np.float32)
    return outp
```

```python
import os
import math
from contextlib import ExitStack
import numpy as np
import ml_dtypes
import concourse.bass as bass
import concourse.mybir as mybir
from concourse.bass_utils import run_bass_kernel_spmd

F32 = mybir.dt.float32
BF16 = mybir.dt.bfloat16
AF = mybir.ActivationFunctionType
ALU = mybir.AluOpType
AX = mybir.AxisListType

NCORES = 8
T = 2048
D = 1024
EPS = 1e-6
DFF = 2816
LN_KS = math.log(128 ** -0.5)


class Prog:
    ENG = ("pe", "act", "dve", "pool", "sp")

    def __init__(self, nc, stack):
        self.nc = nc
        self.NSLOT = 8
        self.sem_sets = [{e: stack.enter_context(nc.semaphore(f"s{k}_{e}")) for e in self.ENG} for k in range(2)]
        self.dsem_sets = [{q: [stack.enter_context(nc.semaphore(f"d{k}_{q}{i}")) for i in range(self.NSLOT)]
                           for q in ("sp", "pool")} for k in range(2)]
        self.cur = 0
        self.sem = self.sem_sets[0]
        self.dsem = self.dsem_sets[0]
        self.cnt = {e: 0 for e in self.ENG}
        self.dcnt = {q: 0 for q in ("sp", "pool")}
        self.ops = {e: [] for e in self.ENG}
        self.seen = {e: {} for e in self.ENG}
        self.lastw = {}
        self.readers = {}
        self.extra_tokens = []

    def _wait(self, e, tok):
        name, sem, val = tok
        if e == "pe" and name == "c_pe":
            return
        if self.seen[e].get(name, 0) >= val:
            return
        self.seen[e][name] = val
        self.ops[e].append(lambda eng, sem=sem, val=val: eng.wait_ge(sem, val))

    def _deps(self, e, reads, writes):
        for k in reads:
            t = self.lastw.get(k)
            if t is not None:
                self._wait(e, t)
        own = f"c_{e}"
        for k in writes:
            t = self.lastw.get(k)
            if t is not None and t[0] != own:
                self._wait(e, t)
            for t in self.readers.get(k, {}).values():
                if t[0] != own:
                    self._wait(e, t)

    def _commit(self, tok, reads, writes):
        for k in reads:
            r = self.readers.setdefault(k, {})
            o = r.get(tok[0])
            if o is None or o[2] < tok[2]:
                r[tok[0]] = tok
        for k in writes:
            self.lastw[k] = tok
            self.readers[k] = {}

    def op(self, e, fn, reads=(), writes=()):
        self._deps(e, reads, writes)
        self.cnt[e] += 1
        sem = self.sem[e]
        tok = (f"c_{e}", sem, self.cnt[e])
        self.ops[e].append(lambda eng, fn=fn, sem=sem: fn(eng).then_inc(sem, 1))
        self._commit(tok, reads, writes)
        return tok

    def dma(self, q, out, in_, reads=(), writes=(), **kw):
        n = self.dcnt[q]
        slot = n % self.NSLOT
        rnd = n // self.NSLOT
        sem = self.dsem[q][slot]
        name = f"d_{q}{slot}"
        if rnd > 0:
            self._wait(q, (name, sem, 16 * rnd))
        self._deps(q, reads, writes)
        self.dcnt[q] += 1
        tok = (name, sem, 16 * (rnd + 1))
        self.ops[q].append(lambda eng, out=out, in_=in_, sem=sem, kw=kw:
                           eng.dma_start(out=out, in_=in_, **kw).then_inc(sem, 16))
        self._commit(tok, reads, writes)
        return tok

    def custom(self, e, fn, tok_sem_name, sem, val, reads=(), writes=()):
        self._deps(e, reads, writes)
        tok = (tok_sem_name, sem, val)
        self.ops[e].append(fn)
        self._commit(tok, reads, writes)
        return tok

    def emit(self, name):
        nc = self.nc
        for q in ("sp", "pool"):
            n = self.dcnt[q]
            for slot in range(self.NSLOT):
                uses = (n - slot + self.NSLOT - 1) // self.NSLOT if n > slot else 0
                if uses > 0:
                    self._wait("sp", (f"d_{q}{slot}", self.dsem[q][slot], 16 * uses))
        for e in ("pe", "act", "dve", "pool"):
            if self.cnt[e] > 0:
                self._wait("sp", (f"c_{e}", self.sem[e], self.cnt[e]))
        for t in self.extra_tokens:
            self._wait("sp", t)
        if os.environ.get("KDEBUG"):
            print("stage", name, "counts", self.cnt, self.dcnt)
        ops = self.ops
        other = 1 - self.cur
        clr = [self.sem_sets[other][e] for e in self.ENG]
        for q in ("sp", "pool"):
            clr += self.dsem_sets[other][q]
        ops["sp"] = [(lambda eng, sm=sm: eng.sem_clear(sm)) for sm in clr] + ops["sp"]
        with nc.Block(name) as block:
            @block.tensor
            def _(eng):
                for f in ops["pe"]:
                    f(eng)

            @block.scalar
            def _(eng):
                for f in ops["act"]:
                    f(eng)

            @block.vector
            def _(eng):
                for f in ops["dve"]:
                    f(eng)

            @block.gpsimd
            def _(eng):
                for f in ops["pool"]:
                    f(eng)

            @block.sync
            def _(eng):
                for f in ops["sp"]:
                    f(eng)
        self.ops = {e: [] for e in self.ENG}
        self.lastw = {}
        self.readers = {}
        self.cur = other
        self.sem = self.sem_sets[other]
        self.dsem = self.dsem_sets[other]
        self.cnt = {e: 0 for e in self.ENG}
        self.dcnt = {q: 0 for q in ("sp", "pool")}
        self.seen = {e: {} for e in self.ENG}
        self.extra_tokens = []


def build(nstage=99, dbg=()):
    nc = bass.Bass("TRN2", target_bir_lowering=False)

    def din(name, shape, dt=F32):
        return nc.dram_tensor(name, list(shape), dt, kind="ExternalInput").ap()

    x_ext = din("x_ext", [4 * T, D])
    c_col = din("c_col", [128, 8])
    w_ada = din("w_ada", [D, 6 * D])
    b_ada = din("b_ada", [1, 6 * D])
    gmix_col = din("gmix_col", [128, 8])
    gffn_col = din("gffn_col", [128, 8])
    w_in = din("w_in", [D, 3592])
    wconv_col = din("wconv_col", [128, 8, 4])
    bconv_col = din("bconv_col", [128, 8])
    bgate_bc = din("bgate_bc", [128, 8])
    gq_col = din("gq_col", [128, 1])
    gk_col = din("gk_col", [128, 1])
    gml_bc = din("gml_bc", [128, 512])
    w_out = din("w_out", [D, D])
    w_gate = din("w_gate", [D, DFF])
    w_up = din("w_up", [D, DFF])
    w_down = din("w_down", [DFF, D])
    ident_d = din("ident", [128, 128], BF16)
    masks_d = din("masks", [128, 12, 128], BF16)
    bd_d = din("bd", [128, 128], BF16)
    trineg_d = din("trineg", [128, 256])
    valid_d = din("valid", [128, 1])
    bvalid_d = din("bvalid", [128, 3])
    out = nc.dram_tensor("out", [T, D], F32, kind="ExternalOutput").ap()
    dbg_out = {}
    for nm, shp, dt_ in dbg:
        dbg_out[nm] = nc.dram_tensor("dbg_" + nm, list(shp), dt_, kind="ExternalOutput").ap()
    modrow_d = nc.dram_tensor("modrow_d", [1, 6 * D], F32, kind="Internal").ap()

    with ExitStack() as top:
        P = Prog(nc, top)

        def sbt(stack, name, shape, dt):
            return stack.enter_context(nc.sbuf_tensor(name, list(shape), dt))

        def pst(stack, name, shape, dt):
            return stack.enter_context(nc.psum_tensor(name, list(shape), dt))

        ident = sbt(top, "ident_s", [128, 128], BF16)
        masks = sbt(top, "masks_s", [128, 12, 128], BF16)
        bd = sbt(top, "bd_s", [128, 128], BF16)
        trineg = sbt(top, "trineg_s", [128, 256], F32)
        valid = sbt(top, "valid_s", [128, 1], F32)
        modc = sbt(top, "modc", [128, 48], F32)
        am = sbt(top, "am", [128, 8], F32)
        af_ = sbt(top, "af", [128, 8], F32)
        gq = sbt(top, "gq", [128, 1], F32)
        gk = sbt(top, "gk", [128, 1], F32)
        attnT = sbt(top, "attnT", [128, 4, T], BF16)
        hmT = sbt(top, "hmT", [128, 4, T], BF16)
        epsc = sbt(top, "epsc", [128, 1], F32)

        def dump(nm, src_ap, key):
            if nm in dbg_out:
                P.dma("sp", dbg_out[nm], src_ap, reads=[key])

        cast_i = [0]

        def load_w(stack_stage, dst_fn, src_fn, nparts, width, keyfn, stg):
            for i in range(nparts):
                s = stg[cast_i[0] % len(stg)]
                skey = ("stg", s.name)
                P.dma("sp", s[:, 0:width], src_fn(i), writes=[skey])
                e = ("act", "dve")[cast_i[0] % 2]
                dst = dst_fn(i)
                if e in ("pool", "dve"):
                    P.op(e, lambda eng, dst=dst, s=s: eng.tensor_copy(out=dst, in_=s[:, 0:width]),
                         reads=[skey], writes=[keyfn(i)])
                else:
                    P.op("act", lambda eng, dst=dst, s=s: eng.activation(out=dst, in_=s[:, 0:width], func=AF.Copy),
                         reads=[skey], writes=[keyfn(i)])
                cast_i[0] += 1

        def make_hT_A(src_rows_ap, bufs, idx):
            nx = len(bufs["xs"])
            xt = bufs["xt"][idx % 2]
            xs = bufs["xs"][idx % nx]
            st = bufs["st"][idx % nx]
            kx, ks, kst = ("xt", idx % 2), ("xs", idx % nx), ("st", idx % nx)
            if src_rows_ap is not None:
                P.dma("sp", xt[:, :], src_rows_ap, writes=[kx])
            P.op("act", lambda eng: eng.activation(out=xs[:, :], in_=xt[:, :], func=AF.Square,
                                                    accum_out=st[:, 0:1]), reads=[kx], writes=[ks, kst])
            P.op("act", lambda eng: eng.activation(out=st[:, 1:2], in_=st[:, 0:1], func=AF.Ln,
                                                    scale=1.0 / D, bias=epsc[:, 0:1]), reads=[kst], writes=[kst])
            P.op("act", lambda eng: eng.activation(out=st[:, 2:3], in_=st[:, 1:2], func=AF.Exp, scale=-0.5),
                 reads=[kst], writes=[kst])
            P.op("dve", lambda eng: eng.tensor_scalar(out=xs[:, :], in0=xt[:, :], scalar1=st[:, 2:3], scalar2=None,
                                                      op0=ALU.mult), reads=[kx, kst], writes=[ks])

        def make_hT_B(a_col, sh_col, dst_fn, dst_key, bufs, idx):
            nx = len(bufs["xs"])
            xs = bufs["xs"][idx % nx]
            psT = bufs["psT"][idx % 2]
            ks, kp = ("xs", idx % nx), ("psT", idx % 2)
            for k in range(8):
                P.op("pe", lambda eng, k=k: eng.transpose(out=psT[:, k, :], in_=xs[:, k * 128:(k + 1) * 128],
                                                          identity=ident[:, :]), reads=[ks], writes=[kp])
            for k in range(8):
                P.op("dve", lambda eng, k=k: eng.tensor_scalar(out=dst_fn(k), in0=psT[:, k, :],
                                                               scalar1=a_col[:, k:k + 1], scalar2=sh_col[:, k:k + 1],
                                                               op0=ALU.mult, op1=ALU.add),
                     reads=[kp], writes=[(dst_key, k)])

        def make_hT(src_rows_ap, a_col, sh_col, dst_fn, dst_key, bufs, idx, x1_dst=None):
            make_hT_A(src_rows_ap, bufs, idx)
            make_hT_B(a_col, sh_col, dst_fn, dst_key, bufs, idx)

        with ExitStack() as st0:
            wst = [sbt(st0, f"wada{i}", [128, 8, 512], F32) for i in range(4)]
            ccol = sbt(st0, "ccol", [128, 8], F32)
            scol = sbt(st0, "scol", [128, 8], F32)
            brow = sbt(st0, "brow", [1, 6 * D], F32)
            mrow = sbt(st0, "mrow", [1, 6 * D], F32)
            gmc = sbt(st0, "gmc", [128, 8], F32)
            gfc = sbt(st0, "gfc", [128, 8], F32)
            onesr = sbt(st0, "onesr", [1, 128], F32)
            noncet = sbt(st0, "noncet", [128, 1], F32)
            ps0 = pst(st0, "ps0", [128, 6, 512], F32)
            psc = pst(st0, "psc", [128, 512], F32)
            for dst, src, key in ((ident[:, :], ident_d, "ident"), (masks[:, :, :], masks_d, "masks"),
                                  (bd[:, :], bd_d, "bd"), (trineg[:, :], trineg_d, "trineg"),
                                  (valid[:, :], valid_d, "valid"), (ccol[:, :], c_col, "ccol"),
                                  (brow[:, :], b_ada, "brow"), (gmc[:, :], gmix_col, "gmc"),
                                  (gfc[:, :], gffn_col, "gfc"), (gq[:, :], gq_col, "gq"), (gk[:, :], gk_col, "gk")):
                P.dma("sp", dst, src, writes=[key])
            P.op("dve", lambda eng: eng.memset(epsc[:, :], EPS), writes=["epsc"])
            nonce_val = float(int.from_bytes(os.urandom(3), "little"))
            P.op("dve", lambda eng: eng.memset(noncet[:, :], nonce_val), writes=["noncet"])
            P.op("dve", lambda eng: eng.memset(onesr[:, :], 1.0), writes=["onesr"])
            P.op("act", lambda eng: eng.activation(out=scol[:, :], in_=ccol[:, :], func=AF.Silu),
                 reads=["ccol"], writes=["scol"])
            P.op("dve", lambda eng: eng.tensor_scalar(out=gq[:, :], in0=gq[:, :], scalar1=0.125, scalar2=None,
                                                      op0=ALU.mult), reads=["gq"], writes=["gq"])
            w_ada_v = w_ada.rearrange("(k p) c -> p k c", p=128)
            for n in range(12):
                w = wst[n % 4]
                P.dma("sp", w[:, :, :], w_ada_v[:, :, n * 512:(n + 1) * 512], writes=[("wada", n % 4)])
                hb = n % 6
                for k in range(8):
                    P.op("pe", lambda eng, w=w, k=k, hb=hb: eng.matmul(out=ps0[0:1, hb, :], lhsT=scol[:, k:k + 1],
                                                                       rhs=w[:, k, :], start=(k == 0), stop=(k == 7)),
                         reads=[("wada", n % 4), "scol"], writes=[("ps0", hb)])
                P.op("dve", lambda eng, n=n, hb=hb: eng.tensor_tensor(out=mrow[0:1, n * 512:(n + 1) * 512],
                                                                      in0=ps0[0:1, hb, :],
                                                                      in1=brow[0:1, n * 512:(n + 1) * 512], op=ALU.add),
                     reads=[("ps0", hb), "brow"], writes=["mrow"])
            for cc in range(48):
                P.op("pe", lambda eng, cc=cc: eng.matmul(out=psc[:, cc:cc + 1], lhsT=mrow[0:1, cc * 128:(cc + 1) * 128],
                                                         rhs=onesr[0:1, 0:1], start=True, stop=True),
                     reads=["mrow", "onesr"], writes=["psc"])
            P.op("dve", lambda eng: eng.tensor_copy(out=modc[:, :], in_=psc[:, 0:48]), reads=["psc"], writes=["modc"])
            P.dma("sp", modrow_d, mrow[0:1, :], reads=["mrow"], writes=["modrow_d"])
            P.op("dve", lambda eng: eng.scalar_tensor_tensor(out=am[:, :], in0=modc[:, 8:16], scalar=1.0, in1=gmc[:, :],
                                                             op0=ALU.add, op1=ALU.mult),
                 reads=["modc", "gmc"], writes=["am"])
            P.op("dve", lambda eng: eng.scalar_tensor_tensor(out=af_[:, :], in0=modc[:, 32:40], scalar=1.0, in1=gfc[:, :],
                                                             op0=ALU.add, op1=ALU.mult),
                 reads=["modc", "gfc"], writes=["af"])
            dump("modc", modc[:, :], "modc")
            P.emit("st0")
        shm = modc[:, 0:8]
        shf = modc[:, 24:32]
        if nstage <= 0:
            return nc

        with ExitStack() as sA:
            qT = sbt(sA, "qT", [128, 4, T], BF16)
            kT = sbt(sA, "kT", [128, 4, 2 * T], BF16)
            VT = sbt(sA, "VT", [128, 4, 2 * T], BF16)
            with ExitStack() as s1:
                wA = sbt(s1, "wA", [128, 8, 1536], BF16)
                stg = [sbt(s1, f"stgA{i}", [128, 1536], F32) for i in range(4)]
                bufs = dict(xt=[sbt(s1, f"xt{i}", [128, D], F32) for i in range(2)],
                            xs=[sbt(s1, f"xs{i}", [128, D], BF16) for i in range(2)],
                            st=[sbt(s1, f"st{i}", [128, 4], F32) for i in range(2)],
                            psT=[pst(s1, f"psT{i}", [128, 8, 128], BF16) for i in range(2)])
                hTg = [sbt(s1, f"hTg{i}", [128, 8, 512], BF16) for i in range(2)]
                sq = [sbt(s1, f"sq{i}", [128, 512], BF16) for i in range(2)]
                rt = [sbt(s1, f"rt{i}", [128, 512], F32) for i in range(2)]
                psq = [pst(s1, f"psq{i}", [128, 512], F32) for i in range(4)]
                pss = [pst(s1, f"pss{i}", [128, 512], F32) for i in range(1)] * 2
                load_w(s1, lambda i: wA[:, i, :], lambda i: w_in[i * 128:(i + 1) * 128, 0:1536], 8, 1536,
                       lambda i: ("wA", i), stg)
                ti = 0
                ci = 0
                pendA = []
                for g in range(8):
                    hg = hTg[g % 2]
                    hkey = ("hTg", g % 2)
                    for j in range(4):
                        r0 = 2 * T + g * 512 + j * 128
                        make_hT(x_ext[r0:r0 + 128, :], am, shm,
                                lambda k, hg=hg, j=j: hg[:, k, j * 128:(j + 1) * 128], hkey, bufs, ti)
                        ti += 1
                    chunks = range(4, 12) if g < 4 else range(12)
                    for ch in chunks:
                        ps = psq[ci % 4]
                        pk = ("psq", ci % 4)
                        for k in range(8):
                            P.op("pe", lambda eng, ps=ps, k=k, ch=ch, hg=hg: eng.matmul(
                                out=ps[:, :], lhsT=wA[:, k, ch * 128:(ch + 1) * 128], rhs=hg[:, k, :],
                                start=(k == 0), stop=(k == 7)), reads=[("wA", k), (hkey, k)], writes=[pk])
                        if ch >= 8:
                            dst = VT[:, ch - 8, g * 512:(g + 1) * 512]
                            P.op("act", lambda eng, ps=ps, dst=dst: eng.activation(out=dst, in_=ps[:, :], func=AF.Copy),
                                 reads=[pk], writes=["VT"])
                        else:
                            s_ = sq[ci % 2]
                            r_ = rt[ci % 2]
                            p2 = pss[ci % 2]
                            ksq, krt, kp2 = ("sq", ci % 2), ("rt", ci % 2), ("pss", 0)
                            P.op("act", lambda eng, ps=ps, s_=s_: eng.activation(out=s_[:, :], in_=ps[:, :], func=AF.Square),
                                 reads=[pk], writes=[ksq])
                            if ch < 4:
                                dst = qT[:, ch, (g - 4) * 512:(g - 3) * 512]
                                gcol = gq
                                dk = "qT"
                            else:
                                dst = kT[:, ch - 4, g * 512:(g + 1) * 512]
                                gcol = gk
                                dk = "kT"

                            def tailA(ps=ps, s_=s_, r_=r_, p2=p2, ksq=ksq, krt=krt, kp2=kp2, pk=pk, dst=dst, gcol=gcol, dk=dk):
                                P.op("pe", lambda eng: eng.matmul(out=p2[:, :], lhsT=bd[:, :], rhs=s_[:, :], start=True, stop=True),
                                     reads=[ksq, "bd"], writes=[kp2])
                                P.op("act", lambda eng: eng.activation(out=r_[:, :], in_=p2[:, :], func=AF.Ln,
                                                                       scale=1.0 / 64, bias=epsc[:, 0:1]),
                                     reads=[kp2], writes=[krt])
                                P.op("act", lambda eng: eng.activation(out=r_[:, :], in_=r_[:, :], func=AF.Exp, scale=-0.5),
                                     reads=[krt], writes=[krt])
                                P.op("dve", lambda eng: eng.scalar_tensor_tensor(
                                    out=dst, in0=ps[:, :], scalar=gcol[:, 0:1], in1=r_[:, :], op0=ALU.mult, op1=ALU.mult),
                                     reads=[pk, krt], writes=[dk])
                            pendA.append(tailA)
                        while len(pendA) > (1 if ch < 8 else 0):
                            pendA.pop(0)()
                        ci += 1
                dump("qT", qT[:, :, :], "qT")
                dump("kT", kT[:, :, :], "kT")
                dump("VT", VT[:, :, :], "VT")
                P.emit("stA")
            if nstage <= 1:
                return nc

            with ExitStack() as s2:
                tiles = []
                for d in (1, 4, 16):
                    for r in range(d):
                        for i in range(-1, 16 // d):
                            tiles.append((d, r, i))
                vt_idx = {t: n for n, t in enumerate(tiles)}
                NVT = len(tiles)
                Vaug = sbt(s2, "Vaug", [128, 2, NVT, 128], BF16)
                accs = [sbt(s2, f"acc{i}", [128, T], F32) for i in range(2)]
                pT = [sbt(s2, f"pT{i}", [128, 4, 128], BF16) for i in range(4)]
                psV = [pst(s2, f"psV{i}", [128, 8, 128], BF16) for i in range(1)] * 2
                psS = [pst(s2, f"psS{i}", [128, 4, 128], F32) for i in range(4)]
                psO = [pst(s2, f"psO{i}", [128, 512], F32) for i in range(2)]
                psR = pst(s2, "psR", [128, 512], F32)
                P.op("pool", lambda eng: eng.memset(Vaug[:, :, :, 64:128], 1.0), writes=["Vones"])
                bi = 0
                si = 0
                for hp in range(4):
                    for b0 in range(0, NVT, 4):
                        pv = psV[0]
                        kpv = ("psV", 0)
                        nb = min(4, NVT - b0)
                        for u in range(nb):
                            d, r, i = tiles[b0 + u]
                            base = T + r + d * 128 * i
                            P.op("pe", lambda eng, pv=pv, u=u, base=base, d=d, hp=hp: eng.transpose(
                                out=pv[:, u, :], in_=VT[:, hp, base:base + 127 * d + 1:d], identity=ident[:, :]),
                                 reads=["VT"], writes=[kpv])
                        e = "dve" if bi % 2 == 0 else "act"
                        for hh in range(2):
                            dst = Vaug[:, hh, b0:b0 + nb, 0:64]
                            src = pv[:, 0:nb, hh * 64:(hh + 1) * 64]
                            if e == "dve":
                                P.op("dve", lambda eng, dst=dst, src=src: eng.tensor_copy(out=dst, in_=src),
                                     reads=[kpv], writes=[("Vaug", hh)])
                            else:
                                P.op("act", lambda eng, dst=dst, src=src: eng.activation(out=dst, in_=src, func=AF.Copy),
                                     reads=[kpv], writes=[("Vaug", hh)])
                        bi += 1
                    for hh in range(2):
                        pb = hh * 64
                        acc = accs[hh]
                        kacc = ("acc", hh)
                        batches = []
                        for d in (1, 4):
                            for r in range(d):
                                for ib in range(0, 16 // d, 2):
                                    batches.append((d, [(r, ib), (r, ib + 1)], 0 if ib == 0 else 4))
                        for r in range(0, 16, 2):
                            batches.append((16, [(r, 0), (r + 1, 0)], 8))
                        acc3 = acc[:, :].rearrange("p (l c) -> p c l", c=16)
                        def emit_S(bidx, d, units, mslot, pb=pb, hp=hp):
                            pS = psS[bidx % 4]
                            kS = ("psS", bidx % 4)
                            for u, (r, i) in enumerate(units):
                                q0 = r + d * 128 * i
                                qsl = slice(q0, q0 + 127 * d + 1, d)
                                kc = slice(T + q0, T + q0 + 127 * d + 1, d)
                                kp_ = slice(T + q0 - 128 * d, T + q0 - d + 1, d)
                                P.op("pe", lambda eng, pS=pS, kp_=kp_, qsl=qsl, u=u, pb=pb, hp=hp: eng.matmul(
                                    out=pS[:, 2 * u, :], lhsT=kT[pb:pb + 64, hp, kp_], rhs=qT[pb:pb + 64, hp, qsl],
                                    start=True, stop=True), reads=["kT", "qT"], writes=[kS])
                                P.op("pe", lambda eng, pS=pS, kc=kc, qsl=qsl, u=u, pb=pb, hp=hp: eng.matmul(
                                    out=pS[:, 2 * u + 1, :], lhsT=kT[pb:pb + 64, hp, kc], rhs=qT[pb:pb + 64, hp, qsl],
                                    start=True, stop=True), reads=["kT", "qT"], writes=[kS])

                        def emit_rest(bidx, d, units, mslot, hh=hh):
                            pS = psS[bidx % 4]
                            p_ = pT[bidx % 4]
                            pO = psO[bidx % 2]
                            kS, kP, kO = ("psS", bidx % 4), ("pT", bidx % 4), ("psO", bidx % 2)
                            P.op("act", lambda eng, pS=pS, p_=p_: eng.activation(out=p_[:, :, :], in_=pS[:, :, :], func=AF.Exp),
                                 reads=[kS], writes=[kP])
                            mk = masks[:, mslot:mslot + 4, :]
                            P.op("dve", lambda eng, p_=p_, mk=mk: eng.tensor_tensor(out=p_[:, :, :], in0=p_[:, :, :], in1=mk, op=ALU.mult),
                                 reads=[kP, "masks"], writes=[kP])
                            for u, (r, i) in enumerate(units):
                                v0 = vt_idx[(d, r, i - 1)]
                                v1 = vt_idx[(d, r, i)]
                                P.op("pe", lambda eng, pO=pO, p_=p_, v0=v0, u=u, hh=hh: eng.matmul(
                                    out=pO[:, u * 128:(u + 1) * 128], lhsT=Vaug[:, hh, v0, :], rhs=p_[:, 2 * u, :], start=True, stop=False),
                                     reads=[kP, ("Vaug", hh), "Vones"], writes=[kO])
                                P.op("pe", lambda eng, pO=pO, p_=p_, v1=v1, u=u, hh=hh: eng.matmul(
                                    out=pO[:, u * 128:(u + 1) * 128], lhsT=Vaug[:, hh, v1, :], rhs=p_[:, 2 * u + 1, :], start=False, stop=True),
                                     reads=[kP, ("Vaug", hh), "Vones"], writes=[kO])
                            r0_, i0_ = units[0]
                            q0 = r0_ + d * 128 * i0_
                            if d == 1:
                                P.op("dve", lambda eng, pO=pO, q0=q0, acc=acc: eng.tensor_copy(out=acc[:, q0:q0 + 256], in_=pO[:, 0:256]),
                                     reads=[kO], writes=[kacc])
                            elif d == 4:
                                asl = slice(q0, q0 + 255 * 4 + 1, 4)
                                P.op("dve", lambda eng, pO=pO, asl=asl, acc=acc: eng.tensor_tensor(
                                    out=acc[:, asl], in0=acc[:, asl], in1=pO[:, 0:256], op=ALU.add),
                                     reads=[kO, kacc], writes=[kacc])
                            else:
                                a3 = acc3[:, r0_:r0_ + 2, :]
                                P.op("dve", lambda eng, pO=pO, a3=a3: eng.tensor_tensor(
                                    out=a3, in0=a3, in1=pO[:, 0:256].rearrange("p (c l) -> p c l", c=2), op=ALU.add),
                                     reads=[kO, kacc], writes=[kacc])

                        LOOK = 3
                        nb_ = len(batches)
                        for bi_ in range(min(LOOK, nb_)):
                            emit_S(si + bi_, *batches[bi_])
                        for bi_ in range(nb_):
                            emit_rest(si + bi_, *batches[bi_])
                            if bi_ + LOOK < nb_:
                                emit_S(si + bi_ + LOOK, *batches[bi_ + LOOK])
                        si += nb_
                        for cq in range(4):
                            csl = slice(cq * 512, (cq + 1) * 512)
                            P.op("act", lambda eng, csl=csl, acc=acc: eng.activation(out=psR[64:128, :], in_=acc[64:128, csl], func=AF.Ln),
                                 reads=[kacc], writes=["psR"])
                            P.op("act", lambda eng: eng.activation(out=psR[64:128, :], in_=psR[64:128, :], func=AF.Exp, scale=-1.0),
                                 reads=["psR"], writes=["psR"])
                            P.op("dve", lambda eng, pb=pb, hp=hp, csl=csl, acc=acc: eng.tensor_tensor(
                                out=attnT[pb:pb + 64, hp, csl], in0=acc[0:64, csl], in1=psR[64:128, :], op=ALU.mult),
                                 reads=[kacc, "psR"], writes=["attnT"])
                dump("attnT", attnT[:, :, :], "attnT")
                P.emit("stATT")
        if nstage <= 2:
            return nc

        with ExitStack() as sM:
            qmT = sbt(sM, "qmT", [128, 4, T], BF16)
            kmT = sbt(sM, "kmT", [128, 4, T], BF16)
            vaug = sbt(sM, "vaug", [128, 16, 4, 130], BF16)
            kw = sbt(sM, "kw", [128, 16, 4, 128], BF16)
            sog = sbt(sM, "sog", [128, 16, 512], BF16)
            gts = sbt(sM, "gts", [128, 16, 8], F32)
            gr = sbt(sM, "gr", [128, 16, 4], F32)
            gwp = sbt(sM, "gwp", [128, 16, 4], F32)
            gwpp = sbt(sM, "gwpp", [128, 16, 4], F32)
            gdec = sbt(sM, "gdec", [128, 16, 4], F32)
            Cst = sbt(sM, "Cst", [128, 4, 130], F32)
            Cbf = sbt(sM, "Cbf", [128, 4, 130], BF16)
            gml = sbt(sM, "gml", [128, 512], F32)
            with ExitStack() as s3:
                wM = sbt(s3, "wM", [128, 8, 2056], BF16)
                stg = [sbt(s3, f"stgM{i}", [128, 514], F32) for i in range(2)]
                bufs = dict(xt=[sbt(s3, f"xtm{i}", [128, D], F32) for i in range(2)],
                            xs=[sbt(s3, f"xsm{i}", [128, D], BF16) for i in range(4)],
                            st=[sbt(s3, f"stm{i}", [128, 4], F32) for i in range(4)],
                            psT=[pst(s3, f"psTm{i}", [128, 8, 128], BF16) for i in range(2)])
                hTgm = sbt(s3, "hTgm", [128, 8, 512], BF16)
                xraw = sbt(s3, "xraw", [128, 8, 516], BF16)
                diag = sbt(s3, "diag", [128, 8, 4, 128], BF16)
                wcv = sbt(s3, "wcv", [128, 8, 4], F32)
                bcv = sbt(s3, "bcv", [128, 8], F32)
                bgt = sbt(s3, "bgt", [128, 8], F32)
                bval = sbt(s3, "bval", [128, 3], F32)
                identf = sbt(s3, "identf", [128, 128], F32)
                kmtmp = sbt(s3, "kmtmp", [128, 4, 512], BF16)
                vtmp = vaug[:, 0:4, :, :]
                kwtmp = [sbt(s3, f"kwtmp{i}", [128, 128], BF16) for i in range(2)]
                gtmp = sbt(s3, "gtmp", [128, 4, 8], F32)
                grt = sbt(s3, "grt", [128, 4, 4], F32)
                gwpt = sbt(s3, "gwpt", [128, 4, 4], F32)
                gwppt = sbt(s3, "gwppt", [128, 4, 4], F32)
                gdect = sbt(s3, "gdect", [128, 4, 4], F32)
                lf = sbt(s3, "lf", [128, 4, 4], F32)
                bb = sbt(s3, "bb", [128, 4, 8], F32)
                t1 = sbt(s3, "t1", [128, 4, 4], F32)
                psA = [pst(s3, f"psA{i}", [128, 512], F32) for i in range(3)]
                psg = pst(s3, "psg", [128, 512], F32)
                psK = [pst(s3, f"psK{i}", [128, 1024], BF16) for i in range(2)]
                for dst, src, key in ((wcv[:, :, :], wconv_col, "wcv"), (bcv[:, :], bconv_col, "bcv"),
                                      (bgt[:, :], bgate_bc, "bgt"), (gml[:, :], gml_bc, "gml"), (bval[:, :], bvalid_d, "bval")):
                    P.dma("sp", dst, src, writes=[key])
                P.op("dve", lambda eng: eng.tensor_copy(out=identf[:, :], in_=ident[:, :]), writes=["identf"])
                for ch in range(8):
                    for j in range(4):
                        P.op("dve", lambda eng, ch=ch, j=j: eng.tensor_scalar(
                            out=diag[:, ch, j, :], in0=identf[:, :], scalar1=wcv[:, ch, j:j + 1], scalar2=None, op0=ALU.mult),
                             reads=["identf", "wcv"], writes=["diag"])
                P.op("pool", lambda eng: eng.memset(vaug[:, :, :, 128:130], 1.0), writes=["vaug"])
                P.op("pool", lambda eng: eng.memset(Cst[:, :, :], 0.0), writes=[("Cst", h_) for h_ in range(4)])
                P.op("pool", lambda eng: eng.memset(xraw[:, :, 0:4], 0.0), writes=[("xraw", c_) for c_ in range(8)])
                load_w(s3, lambda i: wM[:, i // 4, (i % 4) * 514:(i % 4 + 1) * 514],
                       lambda i: w_in[(i // 4) * 128:(i // 4 + 1) * 128, 1536 + (i % 4) * 514:1536 + (i % 4 + 1) * 514], 32, 514,
                       lambda i: ("wM", i // 4), stg)
                ks = 128 ** -0.5
                ti = 0
                ai_box = [0]
                pendM = []
                pendS = []
                kwi = 0
                kwi2 = 0
                NPG = int(os.environ.get("MA_PREFIX_GROUPS", "12"))
                for g in range(12 - NPG, 16):
                    main = g >= 12
                    hg = hTgm
                    hkey = "hTgm"
                    if g == 12 - NPG:
                        for j in range(4):
                            make_hT_A(x_ext[g * 512 + j * 128:g * 512 + (j + 1) * 128, :], bufs, g * 4 + j)
                    for j in range(4):
                        make_hT_B(am, shm, lambda k, j=j: hg[:, k, j * 128:(j + 1) * 128], hkey, bufs, g * 4 + j)
                    chs = list(range(8) if g >= 11 else range(4, 8))
                    for cidx, ch in enumerate(chs):
                        if cidx < 4 and g + 1 < 16:
                            r0n = (g + 1) * 512 + cidx * 128
                            make_hT_A(x_ext[r0n:r0n + 128, :], bufs, (g + 1) * 4 + cidx)
                        ps = psA[ai_box[0] % 3]
                        pk = ("psA", ai_box[0] % 3)
                        ai_box[0] += 1
                        for k in range(8):
                            P.op("pe", lambda eng, ps=ps, k=k, ch=ch: eng.matmul(
                                out=ps[:, :], lhsT=wM[:, k, ch * 128:(ch + 1) * 128], rhs=hg[:, k, :],
                                start=(k == 0), stop=(k == 7)), reads=[("wM", k), (hkey, k)], writes=[pk])
                        P.op("dve", lambda eng, ps=ps, ch=ch: eng.tensor_copy(out=xraw[:, ch, 3:515], in_=ps[:, :]),
                             reads=[pk], writes=[("xraw", ch)])
                        def tailM(ch=ch, g=g, main=main):
                            nonlocal_ai = ai_box
                            if main or ch >= 4:
                                ps2 = psA[nonlocal_ai[0] % 3]
                                pk2 = ("psA", nonlocal_ai[0] % 3)
                                nonlocal_ai[0] += 1
                                for j in range(4):
                                    P.op("pe", lambda eng, ps2=ps2, ch=ch, j=j: eng.matmul(
                                        out=ps2[:, :], lhsT=diag[:, ch, j, :], rhs=xraw[:, ch, j:j + 512],
                                        start=(j == 0), stop=(j == 3)), reads=["diag", ("xraw", ch)], writes=[pk2])
                                if main:
                                    dstT = qmT if ch < 4 else kmT
                                    dst = dstT[:, ch % 4, (g - 12) * 512:(g - 11) * 512]
                                    dkey = "qmT" if ch < 4 else "kmT"
                                else:
                                    dst = kmtmp[:, ch % 4, :]
                                    dkey = "kmtmp"
                                P.op("act", lambda eng, ps2=ps2, dst=dst, ch=ch: eng.activation(
                                    out=dst, in_=ps2[:, :], func=AF.Silu, bias=bcv[:, ch:ch + 1]),
                                     reads=[pk2, "bcv"], writes=[dkey])
                            if g % 4 == 3 and g < 12:
                                P.op("dve", lambda eng, ch=ch, g=g: eng.tensor_scalar(
                                    out=xraw[:, ch, 0:3], in0=xraw[:, ch, 512:515], scalar1=bval[:, g // 4:g // 4 + 1],
                                    scalar2=None, op0=ALU.mult), reads=[("xraw", ch), "bval"], writes=[("xraw", ch)])
                            else:
                                P.op("pool", lambda eng, ch=ch: eng.tensor_copy(out=xraw[:, ch, 0:3], in_=xraw[:, ch, 512:515]),
                                     reads=[("xraw", ch)], writes=[("xraw", ch)])
                        pendM.append(tailM)
                        while len(pendM) > 1:
                            pendM.pop(0)()
                    while pendM:
                        pendM.pop(0)()
                    for j in range(4):
                        tl = (g - 12) * 4 + j
                        hsl = slice(j * 128, (j + 1) * 128)
                        ps = psA[ai_box[0] % 3]
                        pk = ("psA", ai_box[0] % 3)
                        ai_box[0] += 1
                        for k in range(8):
                            P.op("pe", lambda eng, ps=ps, k=k, hsl=hsl: eng.matmul(
                                out=ps[:, :], lhsT=hg[:, k, hsl], rhs=wM[:, k, 1024:1536],
                                start=(k == 0), stop=(k == 7)), reads=[("wM", k), (hkey, k)], writes=[pk])
                        vdst = vaug[:, tl, :, 0:128] if main else vtmp[:, j, :, 0:128]
                        P.op("dve", lambda eng, ps=ps, vdst=vdst: eng.tensor_copy(
                            out=vdst, in_=ps[:, :].rearrange("p (h e) -> p h e", h=4)),
                             reads=[pk], writes=["vaug"])
                        if main:
                            ps = psA[ai_box[0] % 3]
                            pk = ("psA", ai_box[0] % 3)
                            ai_box[0] += 1
                            for k in range(8):
                                P.op("pe", lambda eng, ps=ps, k=k, hsl=hsl: eng.matmul(
                                    out=ps[:, :], lhsT=hg[:, k, hsl], rhs=wM[:, k, 1536:2048],
                                    start=(k == 0), stop=(k == 7)), reads=[("wM", k), (hkey, k)], writes=[pk])
                            P.op("act", lambda eng, ps=ps, tl=tl: eng.activation(out=sog[:, tl, :], in_=ps[:, :], func=AF.Sigmoid),
                                 reads=[pk], writes=["sog"])
                        for k in range(8):
                            P.op("pe", lambda eng, k=k, hsl=hsl: eng.matmul(
                                out=psg[:, 0:8], lhsT=hg[:, k, hsl], rhs=wM[:, k, 2048:2056],
                                start=(k == 0), stop=(k == 7)), reads=[("wM", k), (hkey, k)], writes=["psg"])
                        gdst = gts[:, tl, :] if main else gtmp[:, j, :]
                        P.op("dve", lambda eng, gdst=gdst: eng.tensor_tensor(out=gdst, in0=psg[:, 0:8], in1=bgt[:, :], op=ALU.add),
                             reads=["psg", "bgt"], writes=["gts" if main else "gtmp"])
                    if main:
                        t0_ = (g - 12) * 4
                        G_, GR, GWP, GWPP, GDEC = (gts[:, t0_:t0_ + 4, :], gr[:, t0_:t0_ + 4, :], gwp[:, t0_:t0_ + 4, :],
                                                   gwpp[:, t0_:t0_ + 4, :], gdec[:, t0_:t0_ + 4, :])
                        gk_ = "gts"
                    else:
                        G_, GR, GWP, GWPP, GDEC = gtmp[:, :, :], grt[:, :, :], gwpt[:, :, :], gwppt[:, :, :], gdect[:, :, :]
                        gk_ = "gtmp"
                    P.op("act", lambda eng, G_=G_: eng.activation(out=lf[:, :, :], in_=G_[:, :, 4:8], func=AF.Exp, scale=-1.0),
                         reads=[gk_], writes=["lf"])
                    P.op("act", lambda eng: eng.activation(out=lf[:, :, :], in_=lf[:, :, :], func=AF.Ln, bias=1.0),
                         reads=["lf"], writes=["lf"])
                    for j in range(4):
                        P.op("pe", lambda eng, j=j: eng.matmul(out=psg[:, 16 + j * 8:16 + j * 8 + 4], lhsT=trineg[:, 0:128],
                                                               rhs=lf[:, j, :], start=True, stop=True),
                             reads=["lf", "trineg"], writes=["psg"])
                        P.op("pe", lambda eng, j=j: eng.matmul(out=psg[:, 16 + j * 8 + 4:16 + j * 8 + 8], lhsT=trineg[:, 128:256],
                                                               rhs=lf[:, j, :], start=True, stop=True),
                             reads=["lf", "trineg"], writes=["psg"])
                    P.op("dve", lambda eng: eng.tensor_copy(out=bb[:, :, :], in_=psg[:, 16:48].rearrange("p (t e) -> p t e", e=8)),
                         reads=["psg"], writes=["bb"])
                    P.op("act", lambda eng, GR=GR: eng.activation(out=GR, in_=bb[:, :, 0:4], func=AF.Exp), reads=["bb"], writes=["g_r"])
                    P.op("act", lambda eng, GDEC=GDEC: eng.activation(out=GDEC, in_=bb[:, :, 4:8], func=AF.Exp), reads=["bb"], writes=["g_dec"])
                    P.op("dve", lambda eng, G_=G_: eng.tensor_tensor(out=t1[:, :, :], in0=G_[:, :, 0:4], in1=bb[:, :, 0:4], op=ALU.subtract),
                         reads=[gk_, "bb"], writes=["t1"])
                    P.op("act", lambda eng, GWP=GWP: eng.activation(out=GWP, in_=t1[:, :, :], func=AF.Exp), reads=["t1"], writes=["g_wp"])
                    P.op("dve", lambda eng: eng.tensor_tensor(out=t1[:, :, :], in0=t1[:, :, :], in1=bb[:, :, 4:8], op=ALU.add),
                         reads=["t1", "bb", "g_wp"], writes=["t1"])
                    P.op("act", lambda eng, GWPP=GWPP: eng.activation(out=GWPP, in_=t1[:, :, :], func=AF.Exp), reads=["t1"], writes=["g_wpp"])
                    P.op("dve", lambda eng, GWP=GWP: eng.tensor_scalar(out=GWP, in0=GWP, scalar1=ks, scalar2=None, op0=ALU.mult),
                         reads=["g_wp"], writes=["g_wp"])
                    if main:
                        P.op("dve", lambda eng, GWPP=GWPP: eng.tensor_scalar(out=GWPP, in0=GWPP, scalar1=ks, scalar2=None, op0=ALU.mult),
                             reads=["g_wpp"], writes=["g_wpp"])
                    else:
                        P.op("dve", lambda eng, GWPP=GWPP, g=g: eng.tensor_scalar(out=GWPP, in0=GWPP, scalar1=bval[:, g // 4:g // 4 + 1],
                                                                                scalar2=ks, op0=ALU.mult, op1=ALU.mult),
                             reads=["g_wpp", "bval"], writes=["g_wpp"])
                    for j in range(4):
                        tl = (g - 12) * 4 + j
                        for h in range(4):
                            pk_ = psK[kwi2 % 2]
                            kpk = ("psK", kwi2 % 2)
                            kwi2 += 1
                            ksrc = kmT[:, h, tl * 128:(tl + 1) * 128] if main else kmtmp[:, h, j * 128:(j + 1) * 128]
                            P.op("pe", lambda eng, ksrc=ksrc, pk_=pk_: eng.transpose(out=pk_[:, 0:128], in_=ksrc, identity=ident[:, :]),
                                 reads=["kmT" if main else "kmtmp"], writes=[kpk])
                            if main:
                                kdst = kw[:, tl, h, :]
                                kkey = "kw"
                            else:
                                kdst = kwtmp[kwi % 2][:, :]
                                kkey = ("kwtmp", kwi % 2)
                                kwi += 1
                            P.op("act", lambda eng, kdst=kdst, GWPP=GWPP, j=j, h=h, pk_=pk_: eng.activation(
                                out=kdst, in_=pk_[:, 0:128], func=AF.Copy, scale=GWPP[:, j, h:h + 1]),
                                 reads=[kpk, "g_wpp"], writes=[kkey])
                            if not main:
                                def tailS(kdst=kdst, kkey=kkey, GDEC=GDEC, j=j, h=h):
                                    ps = psA[ai_box[0] % 3]
                                    pk = ("psA", ai_box[0] % 3)
                                    ai_box[0] += 1
                                    P.op("pe", lambda eng: eng.matmul(
                                        out=ps[:, 0:129], lhsT=kdst, rhs=vtmp[:, j, h, 0:129], start=True, stop=True),
                                         reads=[kkey, "vaug"], writes=[pk])
                                    P.op("dve", lambda eng: eng.scalar_tensor_tensor(
                                        out=Cst[:, h, 0:129], in0=Cst[:, h, 0:129], scalar=GDEC[:, j, h:h + 1], in1=ps[:, 0:129],
                                        op0=ALU.mult, op1=ALU.add), reads=[pk, "g_dec", ("Cst", h)], writes=[("Cst", h)])
                                pendS.append(tailS)
                                while len(pendS) > 1:
                                    pendS.pop(0)()
                    while pendS:
                        pendS.pop(0)()
                    if g == 11:
                        P.op("act", lambda eng: eng.activation(out=Cbf[:, :, :], in_=Cst[:, :, :], func=AF.Copy),
                             reads=[("Cst", h_) for h_ in range(4)], writes=["Cbf"])
                dump("gts", gts[:, :, :].rearrange("p t e -> p (t e)"), "gts")
                P.emit("stMA")
            if nstage <= 4:
                return nc


            with ExitStack() as s5:
                At = [sbt(s5, f"At{i}", [128, 128], BF16) for i in range(2)]
                hmt = [sbt(s5, f"hmt{i}", [128, 128], BF16) for i in range(2)]
                gs = [sbt(s5, f"gs{i}", [128, 512], F32) for i in range(2)]
                sm = [sbt(s5, f"sm{i}", [128, 8, 4], F32) for i in range(2)]
                junk = sbt(s5, "junk", [128, 128], BF16)
                psS = [pst(s5, f"psSm{i}", [128, 512], F32) for i in range(2)]
                psN = pst(s5, "psN", [128, 4, 512], F32)
                psC = pst(s5, "psCm", [128, 512], F32)
                psH = pst(s5, "psH", [128, 1024], BF16)
                n = 0
                for tl in range(int(os.environ.get("MB_TILES", "16"))):
                    tsl = slice(tl * 128, (tl + 1) * 128)
                    g_ = gs[tl % 2]
                    kg = ("gs", tl % 2)
                    s_ = sm[tl % 2]
                    ksm = ("sm", tl % 2)
                    P.op("pool", lambda eng, g_=g_, tl=tl: eng.tensor_tensor(out=g_[:, :], in0=sog[:, tl, :], in1=gml[:, :], op=ALU.mult),
                         reads=["sog", "gml"], writes=[kg])
                    for h in range(4):
                        b = n % 2
                        n += 1
                        pS, A_ = psS[b], At[b]
                        kS, kA, kN = ("psS", b), ("At", b), ("psN", h)
                        P.op("pe", lambda eng, pS=pS, h=h, tsl=tsl: eng.matmul(out=pS[:, 0:128], lhsT=kmT[:, h, tsl], rhs=qmT[:, h, tsl],
                                                                               start=True, stop=True),
                             reads=["kmT", "qmT"], writes=[kS])
                        P.op("dve", lambda eng, pS=pS, A_=A_, tl=tl, h=h: eng.scalar_tensor_tensor(
                            out=A_[:, :], in0=pS[:, 0:128], scalar=gwp[:, tl, h:h + 1], in1=masks[:, 1, :], op0=ALU.mult, op1=ALU.mult),
                             reads=[kS, "gwp", "masks"], writes=[kA])
                        P.op("pe", lambda eng, A_=A_, tl=tl, h=h: eng.matmul(out=psN[:, h, 0:129], lhsT=A_[:, :],
                                                                             rhs=vaug[:, tl, h, 0:129], start=True, stop=False),
                             reads=[kA, "vaug"], writes=[kN])
                        P.op("pe", lambda eng, tsl=tsl, h=h: eng.matmul(out=psN[:, h, 0:129], lhsT=qmT[:, h, tsl],
                                                                        rhs=Cbf[:, h, 0:129], start=False, stop=True),
                             reads=["qmT", ("Cbf", h)], writes=[kN])
                        P.op("pe", lambda eng, tl=tl, h=h: eng.matmul(out=psC[:, 0:129], lhsT=kw[:, tl, h, :],
                                                                      rhs=vaug[:, tl, h, 0:129], start=True, stop=True),
                             reads=["kw", "vaug"], writes=["psC"])
                        P.op("dve", lambda eng, tl=tl, h=h: eng.scalar_tensor_tensor(
                            out=Cst[:, h, 0:129], in0=Cst[:, h, 0:129], scalar=gdec[:, tl, h:h + 1], in1=psC[:, 0:129],
                            op0=ALU.mult, op1=ALU.add), reads=["psC", "gdec", ("Cst", h)], writes=[("Cst", h)])
                        P.op("act", lambda eng, h=h: eng.activation(out=Cbf[:, h, 0:129], in_=Cst[:, h, 0:129], func=AF.Copy),
                             reads=[("Cst", h)], writes=[("Cbf", h)])
                        P.op("act", lambda eng, s_=s_, h=h: eng.activation(out=junk[:, :], in_=psN[:, h, 0:128], func=AF.Square,
                                                                           accum_out=s_[:, 4, h:h + 1]),
                             reads=[ksm], writes=[ksm, "junk", kN])
                    kNall = [("psN", h_) for h_ in range(4)]
                    P.op("dve", lambda eng, s_=s_, tl=tl: eng.tensor_tensor(out=s_[:, 0, :], in0=psN[:, :, 128], in1=gr[:, tl, :], op=ALU.mult),
                         reads=kNall + ["gr", ksm], writes=[ksm])
                    P.op("dve", lambda eng, s_=s_: eng.tensor_scalar(out=s_[:, 1, :], in0=s_[:, 0, :], scalar1=-1.0, scalar2=None, op0=ALU.mult),
                         reads=[ksm], writes=[ksm])
                    P.op("dve", lambda eng, s_=s_: eng.scalar_tensor_tensor(out=s_[:, 1, :], in0=s_[:, 0, :], scalar=1.0, in1=s_[:, 1, :],
                                                                            op0=ALU.max, op1=ALU.max), reads=[ksm], writes=[ksm])
                    P.op("dve", lambda eng, s_=s_: eng.reciprocal(out=s_[:, 2, :], in_=s_[:, 1, :]), reads=[ksm], writes=[ksm])
                    P.op("dve", lambda eng, s_=s_, tl=tl: eng.tensor_tensor(out=s_[:, 3, :], in0=s_[:, 2, :], in1=gr[:, tl, :], op=ALU.mult),
                         reads=[ksm, "gr"], writes=[ksm])
                    P.op("dve", lambda eng, s_=s_: eng.tensor_tensor(out=s_[:, 5, :], in0=s_[:, 3, :], in1=s_[:, 3, :], op=ALU.mult),
                         reads=[ksm], writes=[ksm])
                    P.op("dve", lambda eng, s_=s_: eng.scalar_tensor_tensor(out=s_[:, 5, :], in0=s_[:, 5, :], scalar=1.0 / 128,
                                                                            in1=s_[:, 4, :], op0=ALU.mult, op1=ALU.mult),
                         reads=[ksm], writes=[ksm])
                    P.op("act", lambda eng, s_=s_: eng.activation(out=s_[:, 6, :], in_=s_[:, 5, :], func=AF.Ln, bias=epsc[:, 0:1]),
                         reads=[ksm], writes=[ksm])
                    P.op("act", lambda eng, s_=s_: eng.activation(out=s_[:, 7, :], in_=s_[:, 6, :], func=AF.Exp, scale=-0.5),
                         reads=[ksm], writes=[ksm])
                    P.op("dve", lambda eng, s_=s_: eng.tensor_tensor(out=s_[:, 7, :], in0=s_[:, 7, :], in1=s_[:, 3, :], op=ALU.mult),
                         reads=[ksm], writes=[ksm])
                    for h in range(4):
                        hm_ = hmt[h % 2]
                        kh = ("hmt", h % 2)
                        P.op("dve", lambda eng, s_=s_, hm_=hm_, g_=g_, h=h: eng.scalar_tensor_tensor(
                            out=hm_[:, :], in0=psN[:, h, 0:128], scalar=s_[:, 7, h:h + 1], in1=g_[:, h * 128:(h + 1) * 128],
                            op0=ALU.mult, op1=ALU.mult), reads=[("psN", h), ksm, kg], writes=[kh])
                        P.op("pe", lambda eng, hm_=hm_: eng.transpose(out=psH[:, 0:128], in_=hm_[:, :], identity=ident[:, :]),
                             reads=[kh], writes=["psH"])
                        P.op("act", lambda eng, h=h, tsl=tsl: eng.activation(out=hmT[:, h, tsl], in_=psH[:, 0:128], func=AF.Copy),
                             reads=["psH"], writes=["hmT"])
                dump("hmT", hmT[:, :, :], "hmT")
                P.emit("stMB")
        if nstage <= 5:
            return nc

        with ExitStack() as sF:
            hTf = sbt(sF, "hTf", [128, 8, T], BF16)
            gtbc = sbt(sF, "gtbc", [128, 2, D], F32)
            x1sb = sbt(sF, "x1sb", [128, 16, D], F32)
            with ExitStack() as s6:
                wO = sbt(s6, "wO", [128, 8, D], BF16)
                stg = [sbt(s6, f"stgO{i}", [128, D], F32) for i in range(4)]
                bufs = dict(xt=[sbt(s6, f"xtf{i}", [128, D], F32) for i in range(2)],
                            xs=[sbt(s6, f"xsf{i}", [128, D], BF16) for i in range(2)],
                            st=[sbt(s6, f"stf{i}", [128, 4], F32) for i in range(2)],
                            psT=[pst(s6, f"psTf{i}", [128, 8, 128], BF16) for i in range(2)])
                xin_ = [sbt(s6, f"xin{i}", [128, D], F32) for i in range(2)]
                psY = [pst(s6, f"psY{i}", [128, 2, 512], F32) for i in range(2)]
                grow = sbt(s6, "grow", [1, 2, D], F32)
                onesr2 = sbt(s6, "onesr2", [1, 128], F32)
                P.op("dve", lambda eng: eng.memset(onesr2[:, :], 1.0), writes=["onesr2"])
                P.dma("sp", grow[0:1, 0, :], modrow_d[0:1, 2 * D:3 * D], writes=["grow"])
                P.dma("sp", grow[0:1, 1, :], modrow_d[0:1, 5 * D:6 * D], writes=["grow"])
                for gi in range(2):
                    for hh in range(2):
                        P.op("pe", lambda eng, gi=gi, hh=hh: eng.matmul(out=psY[gi][:, hh, :], lhsT=onesr2[0:1, :],
                                                                        rhs=grow[0:1, gi, hh * 512:(hh + 1) * 512],
                                                                        start=True, stop=True),
                             reads=["grow", "onesr2"], writes=[("psY", gi)])
                    P.op("dve", lambda eng, gi=gi: eng.tensor_copy(out=gtbc[:, gi, :],
                                                                   in_=psY[gi][:, :, :].rearrange("p a b -> p (a b)")),
                         reads=[("psY", gi)], writes=["gtbc"])
                load_w(s6, lambda i: wO[:, i, :], lambda i: w_out[i * 128:(i + 1) * 128, :], 8, D, lambda i: ("wO", i), stg)
                for tl in range(16):
                    tsl = slice(tl * 128, (tl + 1) * 128)
                    b = tl % 2
                    pY = psY[b]
                    kY = ("psY", b)
                    xi = xin_[b]
                    kxi = ("xin", b)
                    P.dma("sp", xi[:, :], x_ext[3 * T + tl * 128:3 * T + (tl + 1) * 128, :], writes=[kxi])
                    for nh in range(2):
                        for c in range(8):
                            src = attnT if c < 4 else hmT
                            P.op("pe", lambda eng, pY=pY, nh=nh, c=c, src=src, tsl=tsl: eng.matmul(
                                out=pY[:, nh, :], lhsT=src[:, c % 4, tsl], rhs=wO[:, c, nh * 512:(nh + 1) * 512],
                                start=(c == 0), stop=(c == 7)), reads=[("wO", c), "attnT", "hmT"], writes=[kY])
                    xt = bufs["xt"][tl % 2]
                    kx = ("xt", tl % 2)
                    P.op("dve", lambda eng, pY=pY, xt=xt: eng.tensor_tensor(out=xt[:, :], in0=pY[:, :, :].rearrange("p a b -> p (a b)"),
                                                                            in1=gtbc[:, 0, :], op=ALU.mult),
                         reads=[kY, "gtbc"], writes=[kx])
                    P.op("pool", lambda eng, xt=xt, xi=xi, tl=tl: eng.tensor_tensor(out=xt[:, :], in0=xt[:, :], in1=xi[:, :], op=ALU.add),
                         reads=[kx, kxi], writes=[kx])
                    P.op("pool", lambda eng, xt=xt, tl=tl: eng.tensor_copy(out=x1sb[:, tl, :], in_=xt[:, :]),
                         reads=[kx], writes=[("x1sb", tl)])
                    if tl > 0:
                        psl = slice((tl - 1) * 128, tl * 128)
                        make_hT(None, af_, shf, lambda k, psl=psl: hTf[:, k, psl], "hTf", bufs, tl - 1)
                make_hT(None, af_, shf, lambda k: hTf[:, k, 15 * 128:16 * 128], "hTf", bufs, 15)
                P.emit("stF1")
            if nstage <= 6:
                return nc
            with ExitStack() as s7:
                parts = [(0, 6), (6, 6), (12, 5), (17, 5)]
                WMAX = 768
                wg = sbt(s7, "wg", [128, 8, WMAX], BF16)
                wu = sbt(s7, "wu", [128, 8, WMAX], BF16)
                wd = sbt(s7, "wd", [128, 6, D], BF16)
                stg = [sbt(s7, f"stgF{i}", [128, D], F32) for i in range(4)]
                sgt = [sbt(s7, f"sgt{i}", [128, 512], F32) for i in range(2)]
                actt = [sbt(s7, f"actt{i}", [128, WMAX], BF16) for i in range(2)]
                actT = [sbt(s7, f"actT{i}", [128, 6, 128], BF16) for i in range(2)]
                ot = [sbt(s7, f"ot{i}", [128, D], F32) for i in range(1)] * 2
                psG = [pst(s7, f"psG{i}", [128, 512], F32) for i in range(2)]
                psU = [pst(s7, f"psU{i}", [128, 512], F32) for i in range(2)]
                psTt = [pst(s7, f"psTt{i}", [128, 8, 128], BF16) for i in range(2)]
                psD = pst(s7, "psD", [128, 2, 512], F32)
                gi = 0
                tti = 0
                pend = []
                pendF = []

                def emit_down(tl, aT, kaT, NF, last):
                    tsl = slice(tl * 128, (tl + 1) * 128)
                    o_ = ot[0]
                    ko = ("ot", 0)
                    for nh in range(2):
                        for f in range(NF):
                            P.op("pe", lambda eng, nh=nh, f=f, aT=aT, NF=NF: eng.matmul(
                                out=psD[:, nh, :], lhsT=aT[:, f, :], rhs=wd[:, f, nh * 512:(nh + 1) * 512],
                                start=(f == 0), stop=(f == NF - 1)), reads=[kaT, ("wd", f)], writes=["psD"])
                    P.op("dve", lambda eng, o_=o_: eng.tensor_tensor(out=o_[:, :], in0=psD[:, :, :].rearrange("p a b -> p (a b)"),
                                                                     in1=gtbc[:, 1, :], op=ALU.mult),
                         reads=["psD", "gtbc"], writes=[ko])
                    P.op("dve", lambda eng, o_=o_, tl=tl: eng.tensor_tensor(out=x1sb[:, tl, :], in0=x1sb[:, tl, :], in1=o_[:, :], op=ALU.add),
                         reads=[ko, ("x1sb", tl)], writes=[("x1sb", tl)])
                    if last:
                        P.dma("sp", out[tsl, :], x1sb[:, tl, :], reads=[("x1sb", tl)], writes=[("out", tl)])

                nparts = int(os.environ.get("F2_PARTS", "4"))
                for pi, (f_lo, NF) in enumerate(parts[:nparts]):
                    h0 = f_lo * 128
                    HW_ = NF * 128
                    last = (pi == len(parts) - 1)
                    load_w(s7, lambda i, HW_=HW_: wg[:, i, 0:HW_], lambda i, h0=h0, HW_=HW_: w_gate[i * 128:(i + 1) * 128, h0:h0 + HW_],
                           8, HW_, lambda i: ("wg", i), stg)
                    load_w(s7, lambda i, HW_=HW_: wu[:, i, 0:HW_], lambda i, h0=h0, HW_=HW_: w_up[i * 128:(i + 1) * 128, h0:h0 + HW_],
                           8, HW_, lambda i: ("wu", i), stg)
                    load_w(s7, lambda i: wd[:, i, :], lambda i, h0=h0: w_down[h0 + i * 128:h0 + (i + 1) * 128, :], NF, D,
                           lambda i: ("wd", i), stg)
                    pieces = [(0, 512), (512, HW_ - 512)]
                    for tl in range(int(os.environ.get("F2_TILES", "16"))):
                        tsl = slice(tl * 128, (tl + 1) * 128)
                        b = tl % 2
                        a_, aT, o_ = actt[b], actT[b], ot[b]
                        ka, kaT, ko = ("actt", b), ("actT", b), ("ot", 0)
                        for (c0, w) in pieces:
                            pG, pU, sg = psG[gi % 2], psU[gi % 2], sgt[gi % 2]
                            kG, kU, ksg = ("psG", gi % 2), ("psU", gi % 2), ("sgt", gi % 2)
                            gi += 1
                            for k in range(8):
                                P.op("pe", lambda eng, pG=pG, k=k, c0=c0, w=w, tsl=tsl: eng.matmul(
                                    out=pG[:, 0:w], lhsT=hTf[:, k, tsl], rhs=wg[:, k, c0:c0 + w], start=(k == 0), stop=(k == 7)),
                                     reads=[("hTf", k), ("wg", k)], writes=[kG])
                            for k in range(8):
                                P.op("pe", lambda eng, pU=pU, k=k, c0=c0, w=w, tsl=tsl: eng.matmul(
                                    out=pU[:, 0:w], lhsT=hTf[:, k, tsl], rhs=wu[:, k, c0:c0 + w], start=(k == 0), stop=(k == 7)),
                                     reads=[("hTf", k), ("wu", k)], writes=[kU])
                            P.op("act", lambda eng, pG=pG, sg=sg, w=w: eng.activation(out=sg[:, 0:w], in_=pG[:, 0:w], func=AF.Silu),
                                 reads=[kG], writes=[ksg])
                            P.op("dve", lambda eng, pU=pU, sg=sg, a_=a_, c0=c0, w=w: eng.tensor_tensor(
                                out=a_[:, c0:c0 + w], in0=sg[:, 0:w], in1=pU[:, 0:w], op=ALU.mult),
                                 reads=[ksg, kU], writes=[ka])
                            nf = w // 128
                            pt = psTt[tti % 2]
                            kpt = ("psTt", tti % 2)
                            tti += 1
                            f0 = c0 // 128

                            def tailF(nf=nf, pt=pt, kpt=kpt, a_=a_, c0=c0, ka=ka, aT=aT, f0=f0, kaT=kaT):
                                for u in range(nf):
                                    P.op("pe", lambda eng, u=u: eng.transpose(
                                        out=pt[:, u, :], in_=a_[:, c0 + u * 128:c0 + (u + 1) * 128], identity=ident[:, :]),
                                         reads=[ka], writes=[kpt])
                                P.op("act", lambda eng: eng.activation(
                                    out=aT[:, f0:f0 + nf, :], in_=pt[:, 0:nf, :], func=AF.Copy), reads=[kpt], writes=[kaT])
                            pendF.append(tailF)
                            while len(pendF) > 2:
                                pendF.pop(0)()
                        pend.append((tl, aT, kaT, NF, last))
                        if len(pend) > 1:
                            emit_down(*pend.pop(0))
                    while pendF:
                        pendF.pop(0)()
                    while pend:
                        emit_down(*pend.pop(0))
                P.emit("stF2")
    return nc


def _consts(core):
    b, j = divmod(core, 4)
    k = np.arange(128)[:, None]
    q = np.arange(128)[None, :]
    cur = (k <= q).astype(np.float32)
    prev = (k >= q).astype(np.float32)
    pf = prev * (1.0 if j > 0 else 0.0)
    masks = np.stack([pf, cur, prev, cur, prev, cur, prev, cur, pf, cur, pf, cur], axis=1).astype(ml_dtypes.bfloat16)
    ident = np.eye(128, dtype=np.float32).astype(ml_dtypes.bfloat16)
    bd = np.kron(np.eye(2), np.ones((64, 64))).astype(ml_dtypes.bfloat16)
    trineg = np.concatenate([-(k <= q).astype(np.float32), -np.ones((128, 128), np.float32)], axis=1)
    valid = np.full((128, 1), 1.0 if j > 0 else 0.0, np.float32)
    bvalid = np.zeros((128, 3), np.float32)
    for blk in range(3):
        if j - 3 + blk >= 0:
            bvalid[:, blk] = 1.0
    return dict(ident=ident, masks=np.ascontiguousarray(masks), bd=bd, trineg=np.ascontiguousarray(trineg), valid=valid, bvalid=bvalid)


def make_in_maps(inputs):
    f = lambda a: np.ascontiguousarray(np.asarray(a, dtype=np.float32))
    x = f(inputs["x"])
    col = lambda v: np.ascontiguousarray(v.reshape(-1, 128).T)
    shared = dict(
        w_ada=f(inputs["w_ada"][0]), b_ada=f(inputs["b_ada"][0]).reshape(1, -1),
        gmix_col=col(f(inputs["g_mix"][0])), gffn_col=col(f(inputs["g_ffn"][0])),
        w_in=f(inputs["w_in"][0]),
        wconv_col=np.ascontiguousarray(f(inputs["w_conv"][0]).reshape(4, 8, 128).transpose(2, 1, 0)),
        bconv_col=col(f(inputs["b_conv"][0])),
        bgate_bc=np.ascontiguousarray(np.broadcast_to(
            np.concatenate([f(inputs["b_igate"][0]), f(inputs["b_fgate"][0])])[None, :], (128, 8))),
        gq_col=np.ascontiguousarray(np.tile(f(inputs["q_norm_g"][0]), 2).reshape(128, 1)),
        gk_col=np.ascontiguousarray(np.tile(f(inputs["k_norm_g"][0]), 2).reshape(128, 1)),
        gml_bc=np.ascontiguousarray(np.broadcast_to(f(inputs["mlstm_norm_g"][0])[None, :], (128, 512))),
        w_out=f(inputs["w_out"][0]), w_gate=f(inputs["w_gate"][0]), w_up=f(inputs["w_up"][0]),
        w_down=f(inputs["w_down"][0]),
    )
    maps = []
    for core in range(NCORES):
        b, j = divmod(core, 4)
        xe = np.zeros((4 * T, D), np.float32)
        xe[3 * T:] = x[b, j * T:(j + 1) * T]
        if j > 0:
            xe[3 * T - j * T:3 * T] = x[b, 0:j * T]
        m = dict(shared)
        m["x_ext"] = xe
        m["c_col"] = col(f(inputs["c"][b]))
        m.update(_consts(core))
        maps.append(m)
    return maps


def kernel(**inputs):
    nc = build()
    maps = make_in_maps(inputs)
    res = run_bass_kernel_spmd(nc, maps, core_ids=list(range(NCORES)))
    outp = np.zeros((2, 4 * T, D), np.float32)
    for core in range(NCORES):
        b, j = divmod(core, 4)
        outp[b, j * T:(j + 1) * T] = np.asarray(res.results[core]["out"], dtype=np.float32)
    return outp
```
